# Optimizing a Trainium2 kernel written in Bass

```python
import jax, jax.numpy as jnp
from jax import lax
import numpy as np

D_MODEL = 2048
BATCH = 4
SEQ = 2048
DEPTH = 2
DEC_BATCH = 8
DEC_SEQ = 8
PAST_LEN = 16384
PAGE_SIZE = 128

CONV_WIDTH = D_MODEL // 4
CONV_K = 3
ATT_WIDTH = D_MODEL // 4
ATT_HEADS_PER_GROUP = 4
ATT_HEAD_DIM = ATT_WIDTH // ATT_HEADS_PER_GROUP
DIL_GROUPS = ((128, 1), (512, 4), (2048, 16))
N_DIL = len(DIL_GROUPS)
N_ATT_HEADS = N_DIL * ATT_HEADS_PER_GROUP
POOL_WIDTH = D_MODEL - CONV_WIDTH - ATT_WIDTH
POOL_WINDOWS = (2, 4, 8, 16)
POOL_GROUP = POOL_WIDTH // len(POOL_WINDOWS)
POOL_BUF = max(POOL_WINDOWS) - 1
A_COLS = 3 * CONV_WIDTH
B_COLS = N_DIL * 3 * ATT_WIDTH
IN_COLS = A_COLS + B_COLS + POOL_WIDTH
D_FF = ((8 * D_MODEL // 3 + 127) // 128) * 128
FFN_CONV_K = 3
RMS_EPS = 1e-6

kernel_name = "hymba_conv_dilattn_pool_decoder_step"


def rms_norm(x, g):
    xf = x.astype(jnp.float32)
    y = xf * lax.rsqrt(jnp.mean(xf * xf, axis=-1, keepdims=True) + RMS_EPS)
    return (y * g.astype(jnp.float32)).astype(x.dtype)


def alibi_slopes():
    h = np.arange(1, N_ATT_HEADS + 1, dtype=np.float32)
    return jnp.asarray(np.power(np.float32(2.0), -8.0 * h / N_ATT_HEADS), dtype=jnp.float32)


def causal_dwconv(u, buf, w):
    k = w.shape[0]
    t = u.shape[1]
    ext = jnp.concatenate([buf.astype(u.dtype), u], axis=1)
    y = ext[:, 0:t] * w[0]
    for j in range(1, k):
        y = y + ext[:, j:j + t] * w[j]
    return y, ext[:, ext.shape[1] - (k - 1):]


def dilated_attention(q, k_ext, v_ext, start, window, dilation, slopes):
    t = q.shape[1]
    offs = jnp.arange(window // dilation + 1, dtype=jnp.int32)
    qf = q.astype(jnp.float32) * (ATT_HEAD_DIM ** -0.5)
    q_pos = start + jnp.arange(t, dtype=jnp.int32)

    def rows(ext, j):
        return lax.dynamic_slice_in_dim(ext, window - j * dilation, t, axis=1).astype(jnp.float32)

    def score(j):
        dist = j * dilation
        s = jnp.einsum("bthd,bthd->bth", qf, rows(k_ext, j)) - slopes * dist.astype(jnp.float32)
        ok = (q_pos - dist) >= 0
        return jnp.where(ok[None, :, None], s, -jnp.inf)

    s = lax.map(score, offs)
    m = jnp.max(s, axis=0)
    p = jnp.exp(s - m)
    l = jnp.sum(p, axis=0)

    def acc_step(acc, inp):
        j, pj = inp
        return acc + pj[..., None] * rows(v_ext, j), None

    acc, _ = lax.scan(acc_step, jnp.zeros(qf.shape, jnp.float32), (offs, p))
    return acc / l[..., None], m + jnp.log(l)


def multiscale_pool(u, buf, start, w_pool, scale):
    b, t, _ = u.shape
    ext = jnp.concatenate([buf.astype(u.dtype), u], axis=1).astype(jnp.float32)
    cs = jnp.pad(jnp.cumsum(ext, axis=1), ((0, 0), (1, 0), (0, 0)))
    pos = start + jnp.arange(t, dtype=jnp.int32)
    means = []
    for gi, w in enumerate(POOL_WINDOWS):
        sl = slice(gi * POOL_GROUP, (gi + 1) * POOL_GROUP)
        hi = cs[:, POOL_BUF + 1:POOL_BUF + 1 + t, sl]
        lo = cs[:, POOL_BUF + 1 - w:POOL_BUF + 1 - w + t, sl]
        cnt = jnp.minimum(w, pos + 1).astype(jnp.float32)
        means.append((hi - lo) / cnt[None, :, None])
    d = (jnp.concatenate(means, axis=-1) - u.astype(jnp.float32)).reshape(b, t, len(POOL_WINDOWS), POOL_GROUP)
    z = jnp.einsum("btgc,gcd->btgd", d, w_pool.astype(jnp.float32)).reshape(b, t, POOL_WIDTH)
    z = z * scale.astype(jnp.float32)
    return z.astype(u.dtype), ext[:, ext.shape[1] - POOL_BUF:].astype(u.dtype)


def decoder_layer(x, bufs, params, slopes, start):
    conv_buf, kv_bufs, pool_buf, ffn_buf = bufs
    (norm1, w_in, conv_a_w, w_out, pool_w, pool_scale, norm2, w_gate, w_up, ffn_conv_w, w_down) = params
    b, t, _ = x.shape
    h = rms_norm(x, norm1)
    proj = h @ w_in

    xa = proj[..., 0:CONV_WIDTH]
    gate_b = proj[..., CONV_WIDTH:2 * CONV_WIDTH]
    gate_c = proj[..., 2 * CONV_WIDTH:3 * CONV_WIDTH]
    cu, new_conv = causal_dwconv(gate_c * xa, conv_buf, conv_a_w)
    ya = gate_b * cu

    att = proj[..., A_COLS:A_COLS + B_COLS].reshape(b, t, N_DIL, 3, ATT_HEADS_PER_GROUP, ATT_HEAD_DIM)
    outs, lses, new_kv = [], [], []
    for g, (win, dil) in enumerate(DIL_GROUPS):
        q = att[:, :, g, 0]
        kv = att[:, :, g, 1:3]
        buf = kv_bufs[g].astype(kv.dtype)
        pad = jnp.zeros((b, win - buf.shape[1], 2, ATT_HEADS_PER_GROUP, ATT_HEAD_DIM), kv.dtype)
        ext = jnp.concatenate([pad, buf, kv], axis=1)
        o, lse = dilated_attention(q, ext[:, :, 0], ext[:, :, 1], start, win, dil,
                                   slopes[g * ATT_HEADS_PER_GROUP:(g + 1) * ATT_HEADS_PER_GROUP])
        outs.append(o)
        lses.append(lse)
        keep = min(win, start + t)
        new_kv.append(ext[:, win + t - keep:])
    alpha = jax.nn.softmax(jnp.stack(lses), axis=0)
    yb = jnp.sum(alpha[..., None] * jnp.stack(outs), axis=0).reshape(b, t, ATT_WIDTH).astype(x.dtype)

    yc, new_pool = multiscale_pool(proj[..., A_COLS + B_COLS:], pool_buf, start, pool_w, pool_scale)

    x = x + jnp.concatenate([ya, yb, yc], axis=-1) @ w_out

    h2 = rms_norm(x, norm2)
    gc, new_ffn = causal_dwconv(h2 @ w_gate, ffn_buf, ffn_conv_w)
    x = x + (jax.nn.silu(gc) * (h2 @ w_up)) @ w_down
    return x, (new_kv[0], new_kv[1], new_kv[2], new_conv, new_pool, new_ffn)


def setup_inputs(seed: int = 0) -> dict:
    key = jax.random.key(seed)
    ks = jax.random.split(key, 24)
    f32 = jnp.float32

    def nrm(k, shape, s):
        return jax.random.normal(k, shape, f32) * s

    hh, dh = ATT_HEADS_PER_GROUP, ATT_HEAD_DIM
    return {
        "x_prompt": nrm(ks[0], (BATCH, SEQ, D_MODEL), 1.0),
        "x_sample": nrm(ks[1], (DEC_BATCH, DEC_SEQ, D_MODEL), 1.0),
        "cache_kv_w128": nrm(ks[2], (DEPTH, DEC_BATCH, min(DIL_GROUPS[0][0], PAST_LEN), 2, hh, dh), 1.0),
        "cache_kv_w512": nrm(ks[3], (DEPTH, DEC_BATCH, min(DIL_GROUPS[1][0], PAST_LEN), 2, hh, dh), 1.0),
        "cache_kv_w2048": nrm(ks[4], (DEPTH, DEC_BATCH, min(DIL_GROUPS[2][0], PAST_LEN), 2, hh, dh), 1.0),
        "state_conv_a": nrm(ks[5], (DEPTH, DEC_BATCH, CONV_K - 1, CONV_WIDTH), 1.0),
        "state_pool": nrm(ks[6], (DEPTH, DEC_BATCH, POOL_BUF, POOL_WIDTH), 1.0),
        "state_ffn_conv": nrm(ks[7], (DEPTH, DEC_BATCH, FFN_CONV_K - 1, D_FF), 1.0),
        "norm1": 1.0 + nrm(ks[8], (DEPTH, D_MODEL), 0.05),
        "w_in": nrm(ks[9], (DEPTH, D_MODEL, IN_COLS), D_MODEL ** -0.5),
        "conv_a_w": nrm(ks[10], (DEPTH, CONV_K, CONV_WIDTH), CONV_K ** -0.5),
        "w_out": nrm(ks[11], (DEPTH, D_MODEL, D_MODEL), D_MODEL ** -0.5),
        "pool_w": nrm(ks[12], (DEPTH, len(POOL_WINDOWS), POOL_GROUP, POOL_GROUP), POOL_GROUP ** -0.5),
        "pool_scale": 1.0 + nrm(ks[13], (DEPTH, POOL_WIDTH), 0.1),
        "norm2": 1.0 + nrm(ks[14], (DEPTH, D_MODEL), 0.05),
        "w_gate": nrm(ks[15], (DEPTH, D_MODEL, D_FF), D_MODEL ** -0.5),
        "w_up": nrm(ks[16], (DEPTH, D_MODEL, D_FF), D_MODEL ** -0.5),
        "ffn_conv_w": nrm(ks[17], (DEPTH, FFN_CONV_K, D_FF), FFN_CONV_K ** -0.5),
        "w_down": nrm(ks[18], (DEPTH, D_FF, D_MODEL), D_FF ** -0.5),
        "final_norm": 1.0 + nrm(ks[19], (D_MODEL,), 0.05),
    }


def reference(x_prompt, x_sample, cache_kv_w128, cache_kv_w512, cache_kv_w2048, state_conv_a, state_pool,
              state_ffn_conv, norm1, w_in, conv_a_w, w_out, pool_w, pool_scale, norm2, w_gate, w_up,
              ffn_conv_w, w_down, final_norm):
    slopes = alibi_slopes()
    kv_caches = (cache_kv_w128, cache_kv_w512, cache_kv_w2048)
    hh, dh = ATT_HEADS_PER_GROUP, ATT_HEAD_DIM
    xp, xs = x_prompt, x_sample
    bp, dt = xp.shape[0], xp.dtype
    new_p, new_s = [], []
    for l in range(DEPTH):
        params = (norm1[l], w_in[l], conv_a_w[l], w_out[l], pool_w[l], pool_scale[l], norm2[l],
                  w_gate[l], w_up[l], ffn_conv_w[l], w_down[l])
        p_bufs = (jnp.zeros((bp, CONV_K - 1, CONV_WIDTH), dt),
                  tuple(jnp.zeros((bp, 0, 2, hh, dh), dt) for _ in DIL_GROUPS),
                  jnp.zeros((bp, POOL_BUF, POOL_WIDTH), dt),
                  jnp.zeros((bp, FFN_CONV_K - 1, D_FF), dt))
        xp, sp = decoder_layer(xp, p_bufs, params, slopes, 0)
        s_bufs = (state_conv_a[l], tuple(c[l] for c in kv_caches), state_pool[l], state_ffn_conv[l])
        xs, ss = decoder_layer(xs, s_bufs, params, slopes, PAST_LEN)
        new_p.append(sp)
        new_s.append(ss)

    def stk(lst, i):
        return jnp.stack([s[i] for s in lst])

    y_prompt = rms_norm(xp, final_norm)
    y_sample = rms_norm(xs, final_norm)
    return (y_prompt, y_sample,
            stk(new_p, 0), stk(new_s, 0), stk(new_p, 1), stk(new_s, 1), stk(new_p, 2), stk(new_s, 2),
            stk(new_p, 3), stk(new_s, 3), stk(new_p, 4), stk(new_s, 4), stk(new_p, 5), stk(new_s, 5))
```

```python
import numpy as np
import concourse.bass as bass
import concourse.mybir as mybir
from concourse.bass_utils import run_bass_kernel_spmd

F32 = mybir.dt.float32
BF16 = mybir.dt.bfloat16
ALU = mybir.AluOpType
AF = mybir.ActivationFunctionType

P = 128
D = 2048
KC = 16
TT = 512
NT = 4
NS = 8
DFF = 5504
FC = 43
IN_COLS = 7168
SEQ = 2048
DEPTH = 2
EPS = 1e-6
FGROUPS = [(0, 11), (11, 22), (22, 33), (33, 43)]


class S:
    def __init__(self, nc, name, owner):
        self.h = nc.semaphore(name).__enter__()
        self.name = name
        self.owner = owner
        self.count = 0


class Sync:
    def __init__(self, nc):
        self.nc = nc
        self.eng = {}
        for name, h in (("pe", nc.tensor), ("act", nc.scalar), ("dve", nc.vector),
                        ("pool", nc.gpsimd), ("sp", nc.sync)):
            self.eng[name] = dict(h=h, sem=None, known={})
        self.last_w = {}
        self.readers = {}
        self.nsem = 0

    def sem(self, name, owner):
        self.nsem += 1
        return S(self.nc, name, owner)

    def new_epoch(self, tag):
        for n in ("pe", "act", "dve"):
            self.eng[n]["sem"] = self.sem(f"{n}_{tag}", n)

    def wait(self, engname, stamp, raw):
        if stamp is None:
            return
        s, v = stamp
        if s.owner == engname and (engname == "pe" or not raw):
            return
        e = self.eng[engname]
        if e["known"].get(s.name, 0) >= v:
            return
        e["h"].wait_ge(s.h, v)
        e["known"][s.name] = v

    def deps(self, engname, reads, writes):
        for k in reads:
            self.wait(engname, self.last_w.get(k), True)
            if isinstance(k, tuple) and k[0] == "ps":
                for st in self.readers.get(k, {}).values():
                    self.wait(engname, st, False)
        for k in writes:
            self.wait(engname, self.last_w.get(k), False)
            for st in self.readers.get(k, {}).values():
                self.wait(engname, st, False)

    def record(self, stamp, reads, writes):
        for k in reads:
            self.readers.setdefault(k, {})[stamp[0].name] = stamp
        for k in writes:
            self.last_w[k] = stamp
            self.readers[k] = {}

    def op(self, engname, fn, reads=(), writes=(), inc=True):
        self.deps(engname, reads, writes)
        e = self.eng[engname]
        ins = fn(e["h"])
        s = e["sem"]
        if inc:
            s.count += 1
            ins.then_inc(s.h, 1)
            stamp = (s, s.count)
        else:
            stamp = (s, s.count + 1)
        self.record(stamp, reads, writes)
        return ins

    def dma(self, qname, out, in_, dsem, reads=(), writes=()):
        self.deps(qname, reads, writes)
        ins = self.eng[qname]["h"].dma_start(out=out, in_=in_)
        dsem.count += 16
        ins.then_inc(dsem.h, 16)
        self.record((dsem, dsem.count), reads, writes)


def m16_of_natural():
    perm = np.zeros(512, np.int64)
    for r in range(4):
        for q in range(128):
            perm[r * 128 + q] = 4 * q + r
    return perm


def build_masks():
    h12 = np.arange(1, 13, dtype=np.float32)
    slopes = np.power(np.float32(2.0), -8.0 * h12 / 12).astype(np.float64)
    M = np.zeros((128, 32, 128), np.float64)
    ik = np.arange(128)[:, None]
    iq = np.arange(128)[None, :]
    for h in range(4):
        s1 = slopes[h]
        d = iq - ik
        M[:, h * 2 + 0, :] = np.where(d >= 0, np.exp(-s1 * d), 0.0)
        d = iq - ik + 128
        M[:, h * 2 + 1, :] = np.where(d <= 128, np.exp(-s1 * d), 0.0)
        s2 = slopes[4 + h]
        sk = ik
        sq = iq
        d = sq - sk
        M[:, 8 + h * 2 + 0, :] = np.where(d >= 0, np.exp(-s2 * 4.0 * d), 0.0)
        d = sq - sk + 128
        M[:, 8 + h * 2 + 1, :] = np.where(d <= 128, np.exp(-s2 * 4.0 * d), 0.0)
        s3 = slopes[8 + h]
        same = (ik % 4) == (iq % 4)
        for dt_ in range(4):
            d = (iq // 4) - (ik // 4) + 32 * dt_
            M[:, 16 + h * 4 + dt_, :] = np.where(same & (d >= 0), np.exp(-s3 * 16.0 * d), 0.0)
    SC = np.zeros((128, 4, 24), np.float64)
    SN = np.zeros((128, 4, 24), np.float64)
    for h in range(4):
        s1, s2, s3 = slopes[h], slopes[4 + h], slopes[8 + h]
        for c in range(128):
            for n in range(8):
                d = 128 + n - c
                if d <= 128:
                    SC[c, h, n] = np.exp(-s1 * d)
            for r in range(4):
                for j in range(2):
                    d = 512 + 4 * j - 4 * c
                    if c >= j:
                        SC[c, h, 8 + 2 * r + j] = np.exp(-s2 * d)
            for n in range(8):
                SC[c, h, 16 + n] = np.exp(-s3 * (2048 - 16 * c))
        for n2 in range(8):
            for n in range(8):
                if n2 <= n:
                    SN[n2, h, n] = np.exp(-s1 * (n - n2))
                    if (n - n2) % 4 == 0:
                        SN[n2, h, 8 + n] = np.exp(-s2 * (n - n2))
                if n2 == n:
                    SN[n2, h, 16 + n] = 1.0
    return M.astype(np.float32), SC.astype(np.float32), SN.astype(np.float32)


class StopBuild(Exception):
    pass


STOP_AFTER = None


DEBUG = False


class Builder:
    def dbg(self, name, ap, shape, dt, keys):
        if not DEBUG:
            return
        d = self.nc.dram_tensor("dbg_" + name, list(shape), dt, kind="ExternalOutput").ap()
        self.sy.dma("sp", d, ap, self.ds("dbg_" + name, True), reads=keys)

    def ds(self, name, out=False):
        if name not in self.dsem:
            self.dsem[name] = self.sy.sem(name, "dma")
            if out:
                self.out_sems.append(self.dsem[name])
        return self.dsem[name]

    def stop(self, name):
        if STOP_AFTER == name:
            raise StopBuild(name)

    def __init__(self):
        nc = bass.Bass("TRN2", target_bir_lowering=False)
        self.nc = nc
        self.sy = Sync(nc)
        self.declare_dram()
        self.alloc()
        self.bank_rr = 0

    def din(self, name, shape, dt=F32):
        return self.nc.dram_tensor(name, list(shape), dt, kind="ExternalInput").ap()

    def dout(self, name, shape, dt=F32):
        return self.nc.dram_tensor(name, list(shape), dt, kind="ExternalOutput").ap()

    def declare_dram(self):
        self.xp = self.din("xp", [SEQ, D])
        self.xs_in = self.din("xs_fm", [P, KC, NS])
        self.w_in = self.din("w_in", [DEPTH, D, IN_COLS])
        self.w_out = self.din("w_out", [DEPTH, D, D])
        self.w_gate = self.din("w_gate", [DEPTH, D, DFF])
        self.w_up = self.din("w_up", [DEPTH, D, DFF])
        self.w_down = self.din("w_down", [DEPTH, DFF, D])
        self.pool_w = self.din("pool_w", [DEPTH, 4, 256, 256])
        self.g12 = self.din("g12", [P, DEPTH, 2, KC])
        self.gfin_bc = self.din("gfin_bc", [P, D])
        self.gbc12 = self.din("gbc12", [DEPTH, 2, P, D])
        self.gfin_fm = self.din("gfin_fm", [P, KC])
        self.caw = self.din("caw", [P, DEPTH, 3, 4])
        self.psc = self.din("psc", [P, DEPTH, 8])
        self.fcw = self.din("fcw", [P, DEPTH, 3, FC])
        self.s_conv = self.din("s_conv", [DEPTH, P, 4, 2])
        self.s_pool = self.din("s_pool", [DEPTH, P, 8, 15])
        self.s_ffn = self.din("s_ffn", [DEPTH, P, FC, 2])
        self.ckT = self.din("ckT", [DEPTH, P, 4 * 1664])
        self.cV = self.din("cV", [DEPTH, P, 13 * 512])
        self.c128 = self.din("c128", [DEPTH, 128, 1024])
        self.c512 = self.din("c512", [DEPTH, 512, 1024])
        self.c2048 = self.din("c2048", [DEPTH, 2048, 1024])
        self.masks_d = self.din("masks", [P, 32 * 128])
        self.smc_d = self.din("smask_c", [P, 4 * 24])
        self.smn_d = self.din("smask_n", [P, 4 * 24])
        self.rc_d = self.din("rc_tab", [P, 16])
        self.ident_d = self.din("ident", [P, P])
        self.y_p = self.dout("y_p", [SEQ, D])
        self.y_s = self.dout("y_s", [P, KC, NS])
        self.kv128_p = self.dout("kv128_p", [DEPTH, 128, 1024])
        self.kv128_s = self.dout("kv128_s", [DEPTH, 128, 1024])
        self.kv512_p = self.dout("kv512_p", [DEPTH, 512, 1024])
        self.kv512_s = self.dout("kv512_s", [DEPTH, 512, 1024])
        self.kv2048_p = self.dout("kv2048_p", [DEPTH, 2048, 1024])
        self.kv2048_s = self.dout("kv2048_s", [DEPTH, 2048, 1024])
        self.conv_p = self.dout("conv_p", [DEPTH, P, 4, 2])
        self.conv_s = self.dout("conv_s", [DEPTH, P, 4, 2])
        self.pool_p = self.dout("pool_p", [DEPTH, P, 8, 15])
        self.pool_s = self.dout("pool_s", [DEPTH, P, 8, 15])
        self.ffn_p = self.dout("ffn_p", [DEPTH, P, FC, 2])
        self.ffn_s = self.dout("ffn_s", [DEPTH, P, FC, 2])
        self.xsc = self.nc.dram_tensor("xsc", [NT, P, 4 * D], F32, kind="Internal").ap()
        self.hsc = self.nc.dram_tensor("hsc", [DEPTH, NT, P, 4096], BF16, kind="Internal").ap()

    def sb(self, name, shape, dt):
        return self.nc.alloc_sbuf_tensor("sb_" + name, list(shape), dt)

    def alloc(self):
        nc = self.nc
        self.x = self.sb("x", [P, 4, D], F32)
        self.kT1 = self.sb("kT1", [P, 4, 640], BF16)
        self.kT2 = self.sb("kT2", [P, 4, 2, 512], BF16)
        self.kT3 = self.sb("kT3", [P, 4, 512], BF16)
        self.V1 = self.sb("V1", [P, 5, 512], BF16)
        self.V2 = self.sb("V2", [P, 2, 4, 512], BF16)
        self.V3 = self.sb("V3", [P, 4, 512], BF16)
        self.ring = [self.sb(f"ring{i}", [P, 8192], BF16) for i in range(3)]
        self.masks = self.sb("masks", [P, 32, 128], BF16)
        self.smc = self.sb("smc", [P, 4, 24], BF16)
        self.smn = self.sb("smn", [P, 4, 24], BF16)
        self.rc = self.sb("rc", [P, 16], F32)
        self.g12_sb = self.sb("g12s", [P, DEPTH, 2, KC], F32)
        self.gfin_sb = self.sb("gfins", [P, KC], F32)
        self.caw_sb = self.sb("caws", [P, DEPTH, 3, 4], F32)
        self.psc_sb = self.sb("pscs", [P, DEPTH, 8], F32)
        self.fcw_sb = self.sb("fcws", [P, DEPTH, 3, FC], F32)
        self.uh = [self.sb(f"uh{i}", [P, 4, 2], F32) for i in range(2)]
        self.ph = [self.sb(f"ph{i}", [P, 8, 15], F32) for i in range(2)]
        self.gh = [self.sb(f"gh{i}", [P, FC, 2], F32) for i in range(2)]
        self.ident = self.sb("ident", [P, P], BF16)
        self.ones = self.sb("ones", [P, P], BF16)
        self.xs = self.sb("xs", [P, KC, NS], F32)
        self.stat = self.sb("stat", [P, 16], F32)
        self.epsb = self.sb("epsb", [P, 1], F32)
        self.hT_s = self.sb("hTs", [P, KC, NS], BF16)
        self.ycat_s = self.sb("ycats", [P, KC, NS], BF16)
        self.qT_s = self.sb("qTs", [P, 12, NS], BF16)
        self.kT_s = self.sb("kTs", [P, 12, NS], BF16)
        self.V_s = self.sb("Vs", [P, 3, 512], BF16)
        self.act_s = self.sb("acts", [P, FC, NS], BF16)
        self.pts = self.sb("pts", [P, 4, 48], BF16)
        self.t_sq = self.sb("t_sq", [P, KC, NS], F32)
        self.t_sqb = self.sb("t_sqb", [P, KC, NS], BF16)
        self.t_lo = self.sb("t_lo", [P, KC, NS], F32)
        self.t_lob = self.sb("t_lob", [P, KC, NS], BF16)
        self.t_rs = self.sb("t_rs", [P, NS], F32)
        self.t_t = self.sb("t_t", [P, NS], F32)
        self.t_yo = self.sb("t_yo", [P, KC, NS], F32)
        self.t_pin = self.sb("t_pin", [P, 2, 23], F32)
        self.t_sA = self.sb("t_sA", [P, 2, 23], F32)
        self.t_sB = self.sb("t_sB", [P, 2, 23], F32)
        self.t_dg = self.sb("t_dg", [P, 2, NS], BF16)
        self.t_st = [self.sb(f"t_st{i}", [P, 2, NS], BF16) for i in range(4)]
        self.t_gc = self.sb("t_gc", [P, 4, NS], F32)
        self.t_gb = self.sb("t_gb", [P, 4, NS], F32)
        self.t_ue = self.sb("t_ue", [P, NS + 2], F32)
        self.t_cu = self.sb("t_cu", [P, NS], F32)
        self.t_gt = self.sb("t_gt", [P, NS + 2], F32)
        self.t_cv = self.sb("t_cv", [P, NS], F32)
        self.t_cv4 = [self.sb(f"t_cv4{i}", [P, NS], F32) for i in range(4)]
        self.t_rl = self.sb("t_rl", [P, NS], F32)
        self.ARENA = 71680
        self.arena = self.sb("arena", [P, self.ARENA // 4], F32)
        self.psb = [nc.alloc_psum_tensor(f"ps{i}", [P, 512], F32) for i in range(8)]

    def av(self, off, nbytes, dt, pattern=None, **kw):
        assert off % 4 == 0 and nbytes % 4 == 0 and off + nbytes <= self.ARENA, (off, nbytes)
        ap = self.arena[:, off // 4:(off + nbytes) // 4]
        if dt == BF16:
            ap = ap.bitcast(BF16)
        if pattern:
            ap = ap.rearrange(pattern, **kw)
        return ap

    @staticmethod
    def pg(off, nbytes):
        return [("A", i) for i in range(off // 1024, (off + nbytes - 1) // 1024 + 1)]

    def next_bank(self):
        b = self.bank_rr
        self.bank_rr = (self.bank_rr + 1) % 8
        return b

    def mm_group(self, bank, mms, extra_w=()):
        sy = self.sy
        key = ("ps", bank)
        sy.deps("pe", (), [key])
        n = len(mms)
        for i, (o, l, r, rk) in enumerate(mms):
            last = i == n - 1
            sy.op("pe", lambda e, o=o, l=l, r=r, i=i, last=last: e.matmul(o, l, r, start=(i == 0), stop=last),
                  reads=rk, writes=([key] + list(extra_w)) if last else (), inc=last)

    def mm_raw(self, bank, o, l, r, rk, start, last_inc):
        sy = self.sy
        key = ("ps", bank)
        if start:
            sy.deps("pe", (), [key])
        sy.op("pe", lambda e: e.matmul(o, l, r, start=start, stop=False, skip_group_check=True),
              reads=rk, writes=[key] if last_inc else (), inc=last_inc)

    def ring_init(self):
        self.rq = []
        self.rq_emitted = 0
        self.rq_taken = 0
        self.rsem = [self.sy.sem(f"ring{i}", "dma") for i in range(3)]

    def ring_declare(self, descs):
        self.rq.extend(descs)

    def ring_emit_load(self, i):
        d = self.rq[i]
        slot = i % 3
        for (dst_lo, n, pattern, kw, src, skeys) in d["parts"]:
            dst = self.ring[slot][:, dst_lo:dst_lo + n]
            if pattern:
                dst = dst.rearrange(pattern, **kw)
            self.sy.dma("pool", dst, src, self.rsem[slot], reads=skeys, writes=[("ring", slot)])

    def ring_get(self, n=1):
        first = self.rq_taken
        assert n <= 2
        while self.rq_emitted < min(len(self.rq), first + 3):
            self.ring_emit_load(self.rq_emitted)
            self.rq_emitted += 1
        out = []
        for i in range(first, first + n):
            slot = i % 3
            out.append((self.ring[slot], [("ring", slot)], self.rq[i]))
        self.rq_taken += n
        return out[0] if n == 1 else out

    @staticmethod
    def wdesc(tag, src3, kc, ncols):
        return dict(tag=tag, kc=kc, ncols=ncols,
                    parts=[(0, kc * ncols, "p (k n) -> p k n", dict(n=ncols), src3, ())])

    def pass_blocks(self, l, t):
        d = []
        win = self.w_in[l].rearrange("(k p) n -> p k n", p=P)
        order = [12, 13, 1, 2, 0] + list(range(3, 12))
        for blk in order:
            d.append(self.wdesc(("in", blk), win[:, :, blk * 512:(blk + 1) * 512], KC, 512))
            if blk == 13:
                pw = self.pool_w[l].rearrange("g (kc p) d -> p g kc d", p=P)
                d.append(dict(tag=("pw",), kc=0, ncols=0,
                              parts=[(0, 2048, "p (g kc d) -> p g kc d", dict(g=4, kc=2), pw, ())]))
        if t >= 1:
            prs = [(0, 1)] if t == 3 else [tuple(range(t))]
            if t == 3:
                prs = [(0, 1), (2,)]
            for grp in prs:
                parts = []
                for j, tp in enumerate(grp):
                    parts.append((j * 4096, 4096, None, None, self.hsc[l, tp], [("hsc", l, tp, 0), ("hsc", l, tp, 1)]))
                d.append(dict(tag=("hist", grp), kc=0, ncols=0, parts=parts))
        if t == 0:
            d.append(dict(tag=("cK",), kc=0, ncols=0, parts=[(0, 6656, None, None, self.ckT[l], ())]))
            d.append(dict(tag=("cV",), kc=0, ncols=0, parts=[(0, 6656, None, None, self.cV[l], ())]))
        wo = self.w_out[l].rearrange("(k p) n -> p k n", p=P)
        for n in range(4):
            d.append(self.wdesc(("out", n), wo[:, :, n * 512:(n + 1) * 512], KC, 512))
        wg = self.w_gate[l].rearrange("(k p) n -> p k n", p=P)
        wu = self.w_up[l].rearrange("(k p) n -> p k n", p=P)
        wd = self.w_down[l].rearrange("(c p) n -> p c n", p=P)

        def gu(gi):
            c0, c1 = FGROUPS[gi]
            out = []
            c = c0
            while c < c1:
                ce = min(c + 4, c1)
                out.append(self.wdesc(("gate", c, ce), wg[:, :, c * 128:ce * 128], KC, (ce - c) * 128))
                out.append(self.wdesc(("up", c, ce), wu[:, :, c * 128:ce * 128], KC, (ce - c) * 128))
                c = ce
            return out

        def dn(gi):
            c0, c1 = FGROUPS[gi]
            return [self.wdesc(("down", gi, n), wd[:, c0:c1, n * 512:(n + 1) * 512], c1 - c0, 512) for n in range(4)]

        d += gu(0) + gu(1) + dn(0) + gu(2) + dn(1) + gu(3) + dn(2) + dn(3)
        return d

    def build(self):
        nc, sy = self.nc, self.sy
        self.ring_init()
        self.dsem = {}
        self.out_sems = []
        sy.new_epoch("init")
        self.prologue()
        for l in range(DEPTH):
            for t in range(NT):
                self.ring_declare(self.pass_blocks(l, t))
        try:
            self.stop("prologue")
            for l in range(DEPTH):
                self.layer_start(l)
                for t in range(NT):
                    sy.new_epoch(f"l{l}t{t}")
                    self.run_pass(l, t)
                    self.stop(f"pass{l}{t}")
                self.layer_end(l)
        except StopBuild:
            pass
        for sm in self.out_sems:
            if sm.count:
                nc.sync.wait_ge(sm.h, sm.count)
        return nc

    def prologue(self):
        sy, ds = self.sy, self.dsem
        sy.dma("pool", self.masks[:].rearrange("p t q -> p (t q)"), self.masks_d, self.ds("miscp"), writes=["masks"])
        sy.dma("pool", self.smc[:].rearrange("p h q -> p (h q)"), self.smc_d, self.ds("miscp"), writes=["smc"])
        sy.dma("pool", self.smn[:].rearrange("p h q -> p (h q)"), self.smn_d, self.ds("miscp"), writes=["smn"])
        sy.dma("pool", self.ident[:], self.ident_d, self.ds("miscp"), writes=["ident"])
        for dst, src, k in ((self.rc, self.rc_d, "rc"), (self.g12_sb, self.g12, "g12"), (self.gfin_sb, self.gfin_fm, "gfin"),
                            (self.caw_sb, self.caw, "caw"), (self.psc_sb, self.psc, "psc"), (self.fcw_sb, self.fcw, "fcw"),
                            (self.xs, self.xs_in, "xs")):
            sy.dma("sp", dst[:], src, self.ds("misc"), writes=[k])
        for k in ("masks", "smc", "smn", "ident"):
            sy.last_w[k] = (self.ds("miscp"), self.ds("miscp").count)
        for k in ("rc", "g12", "gfin", "caw", "psc", "fcw", "xs"):
            sy.last_w[k] = (self.ds("misc"), self.ds("misc").count)
        sy.op("dve", lambda e: e.memset(self.ones[:], 1.0), writes=["ones"])
        sy.op("dve", lambda e: e.memset(self.epsb[:], EPS), writes=["epsb"])
        for tname, ten in (("kT1", self.kT1), ("kT2", self.kT2), ("kT3", self.kT3), ("V1", self.V1), ("V2", self.V2), ("V3", self.V3)):
            sy.op("dve", lambda e, ten=ten: e.memset(ten[:], 0.0), writes=[tname])
        for l in range(DEPTH):
            sy.dma("sp", self.kv128_s[l, 0:120, :], self.c128[l, 8:128, :], self.ds("ocp", True))
            sy.dma("sp", self.kv512_s[l, 0:504, :], self.c512[l, 8:512, :], self.ds("ocp", True))
            sy.dma("sp", self.kv2048_s[l, 0:2040, :], self.c2048[l, 8:2048, :], self.ds("ocp", True))

    def layer_start(self, l):
        sy, ds = self.sy, self.dsem
        for ten, k in ((self.uh[0], "uh0"), (self.ph[0], "ph0"), (self.gh[0], "gh0")):
            sy.op("dve", lambda e, ten=ten: e.memset(ten[:], 0.0), writes=[k])
        sy.dma("sp", self.uh[1][:], self.s_conv[l], self.ds("lds0"), writes=["uh1"])
        sy.dma("sp", self.ph[1][:], self.s_pool[l], self.ds("lds1"), writes=["ph1"])
        sy.dma("sp", self.gh[1][:], self.s_ffn[l], self.ds("lds2"), writes=["gh1"])

    def layer_end(self, l):
        sy, ds = self.sy, self.dsem
        for src, dst, k in ((self.uh[0], self.conv_p[l], "uh0"), (self.ph[0], self.pool_p[l], "ph0"), (self.gh[0], self.ffn_p[l], "gh0"),
                            (self.uh[1], self.conv_s[l], "uh1"), (self.ph[1], self.pool_s[l], "ph1"), (self.gh[1], self.ffn_s[l], "gh1")):
            sy.dma("sp", dst, src[:], self.ds("ost_" + k, True), reads=[k])

    def norm_gbc_load(self, l, which):
        GQ = 32768
        gbc = self.av(GQ, 8192, F32)
        self.sy.dma("sp", gbc[:, :], self.gbc12[l, which], self.ds("ldg"), writes=self.pg(GQ, 8192))

    def norm_tm(self, l, which):
        sy = self.sy
        R0, R1, GQ = 0, 16384, 32768
        hT = self.av(R0, 16384, BF16, "p (k t) -> p k t", t=TT)
        htm = self.av(R1, 16384, BF16, "p (j d) -> p j d", d=D)
        gbc = self.av(GQ, 8192, F32)
        for j in range(4):
            sy.op("act", lambda e, j=j: e.activation(out=htm[:, j, :], in_=self.x[:, j, :], func=AF.Square,
                                                     accum_out=self.stat[:, j:j + 1]),
                  reads=[("x", j)], writes=self.pg(R1 + 4096 * j, 4096) + [("stat", j)])
            sy.op("act", lambda e, j=j: e.activation(out=self.stat[:, 4 + j:5 + j], in_=self.stat[:, j:j + 1], func=AF.Sqrt,
                                                     bias=self.epsb[:, 0:1], scale=1.0 / D),
                  reads=[("stat", j), "epsb"], writes=[("stat", 4 + j)])
            sy.op("dve", lambda e, j=j: e.reciprocal(out=self.stat[:, 8 + j:9 + j], in_=self.stat[:, 4 + j:5 + j]),
                  reads=[("stat", 4 + j)], writes=[("stat", 8 + j)])
            sy.op("dve", lambda e, j=j: e.scalar_tensor_tensor(out=htm[:, j, :], in0=self.x[:, j, :], scalar=self.stat[:, 8 + j:9 + j],
                                                               in1=gbc[:, :], op0=ALU.mult, op1=ALU.mult),
                  reads=[("x", j), ("stat", 8 + j)] + self.pg(GQ, 8192), writes=self.pg(R1 + 4096 * j, 4096))
        for j in range(4):
            for half in range(2):
                b = self.next_bank()
                pst = self.psb[b][:, :].bitcast(BF16)
                key = ("ps", b)
                sy.deps("pe", (), [key])
                for kk in range(8):
                    k = half * 8 + kk
                    last = kk == 7
                    sy.op("pe", lambda e, j=j, k=k, kk=kk, pst=pst: e.transpose(pst[:, kk * 128:(kk + 1) * 128], htm[:, j, k * 128:(k + 1) * 128], self.ident[:]),
                          reads=self.pg(R1 + 4096 * j, 4096) + ["ident"], writes=[key] if last else (), inc=last)
                dst = hT[:, half * 8:half * 8 + 8, j * 128:(j + 1) * 128]
                src = pst.rearrange("p (k t) -> p k t", t=128)
                wkeys = self.pg(R0 + 8192 * half, 8192)
                if (j + half) % 2 == 0:
                    sy.op("act", lambda e, dst=dst, src=src: e.activation(out=dst, in_=src, func=AF.Copy), reads=[key], writes=wkeys)
                else:
                    sy.op("dve", lambda e, dst=dst, src=src: e.tensor_copy(out=dst, in_=src), reads=[key], writes=wkeys)
        return hT

    def norm_sample(self, gsel):
        sy = self.sy
        sq = self.t_sq[:]
        sy.op("dve", lambda e: e.tensor_tensor(out=sq, in0=self.xs[:], in1=self.xs[:], op=ALU.mult),
              reads=["xs"], writes=["tmps_sq"])
        sqb = self.t_sqb[:]
        sy.op("dve", lambda e: e.tensor_copy(out=sqb, in_=sq), reads=["tmps_sq"], writes=["tmps_sqb"])
        lo = self.t_lo[:]
        sy.op("dve", lambda e: e.tensor_tensor(out=lo, in0=sq, in1=sqb, op=ALU.subtract),
              reads=["tmps_sq", "tmps_sqb"], writes=["tmps_lo"])
        lob = self.t_lob[:]
        sy.op("dve", lambda e: e.tensor_copy(out=lob, in_=lo), reads=["tmps_lo"], writes=["tmps_lob"])
        b = self.next_bank()
        mms = []
        for src, kk in ((sqb, "tmps_sqb"), (lob, "tmps_lob")):
            for k in range(KC):
                mms.append((self.psb[b][:, 0:NS], self.ones[:], src[:, k, :], [kk, "ones"]))
        self.mm_group(b, mms)
        rs = self.t_rs[:]
        sy.op("act", lambda e: e.activation(out=self.t_t[:], in_=self.psb[b][:, 0:NS], func=AF.Sqrt, bias=self.epsb[:, 0:1], scale=1.0 / D),
              reads=[("ps", b), "epsb"], writes=["tmps_t"])
        sy.op("dve", lambda e: e.reciprocal(out=rs, in_=self.t_t[:]), reads=["tmps_t"], writes=["tmps_rs"])
        return rs

    def run_pass(self, l, t):
        sy, ds = self.sy, self.dsem
        samp = (t == 0)
        par = t % 2
        if l == 0:
            for j in range(4):
                sy.dma("sp", self.x[:, j, :], self.xp[t * TT + j * P:t * TT + (j + 1) * P, :], self.ds(f"ldx{j}"), writes=[("x", j)])
        else:
            for j in range(4):
                sy.dma("sp", self.x[:, j, :], self.xsc[t, :, j * D:(j + 1) * D], self.ds(f"ldx{j}"),
                       reads=[("xsc", t, j)], writes=[("x", j)])
        self.stop("load")
        self.norm_gbc_load(l, 0)
        hT = self.norm_tm(l, 0)
        if l == 0 and t == 0:
            self.dbg("stat", self.stat[:], [P, 16], F32, [("stat", i) for i in range(12)])
            self.dbg("hT", hT, [P, KC, TT], BF16, self.pg(0, 16384))
            self.dbg("htm", self.av(16384, 16384, BF16, "p (j d) -> p j d", d=D), [P, 4, D], BF16, self.pg(16384, 16384))
        self.stop("norm1")
        hTk = lambda k: self.pg(1024 * k, 1024)
        if samp:
            rs = self.norm_sample(None)
            for k in range(KC):
                sy.op("dve", lambda e, k=k: e.tensor_tensor(out=self.t_t[:], in0=self.xs[:, k, :], in1=rs, op=ALU.mult),
                      reads=["xs", "tmps_rs"], writes=["tmps_t"])
                sy.op("dve", lambda e, k=k: e.tensor_scalar(out=self.hT_s[:, k, :], in0=self.t_t[:],
                                                            scalar1=self.g12_sb[:, l, 0, k:k + 1], scalar2=None, op0=ALU.mult),
                      reads=["tmps_t", "g12"], writes=["hTs"])
        YC, QT, TMP, KVST = 16384, 32768, 45056, 62464
        PT = TMP
        ycat = self.av(YC, 16384, BF16, "p (k t) -> p k t", t=TT)
        qT = self.av(QT, 12288, BF16, "p (k t) -> p k t", t=TT)
        ptbs = [self.av(PT + 8192 * i, 8192, BF16, "p (k t) -> p k t", t=TT) for i in range(2)]
        kvst = [self.av(KVST + 2048 * i, 2048, F32) for i in range(4)]
        self.kvst_i = 0
        evt = [0]

        def ev_eng():
            evt[0] += 1
            return "act" if evt[0] % 2 else "dve"

        def copy_op(eng, out, in_, reads, writes):
            if eng == "act":
                sy.op("act", lambda e: e.activation(out=out, in_=in_, func=AF.Copy), reads=reads, writes=writes)
            else:
                sy.op("dve", lambda e: e.tensor_copy(out=out, in_=in_), reads=reads, writes=writes)

        def fm_block(wb, wk, nchunks, rhs_of_k, rkeys_of_k, N, evac):
            for c in range(nchunks):
                b = self.next_bank()
                mms = [(self.psb[b][:, 0:N], wb[:, k, c * 128:(c + 1) * 128], rhs_of_k(k), wk + rkeys_of_k(k)) for k in range(KC)]
                self.mm_group(b, mms)
                evac(c, b)

        hs_of_k = lambda k: self.hT_s[:, k, :]
        hs_keys = lambda k: ["hTs"]
        hp_of_k = lambda k: hT[:, k, :]

        PIN, SA, SB_, DG = TMP, TMP + 4224, TMP + 8448, TMP + 12672
        pin = self.av(PIN, 4216, F32, "p (c e) -> p c e", e=527)
        sA = self.av(SA, 4216, F32, "p (c e) -> p c e", e=527)
        sB = self.av(SB_, 4216, F32, "p (c e) -> p c e", e=527)
        dg = self.av(DG, 2048, BF16, "p (c t) -> p c t", t=TT)
        pin_s = self.t_pin[:]
        sA_s = self.t_sA[:]
        sB_s = self.t_sB[:]
        dg_s = self.t_dg[:]
        pending_pool = []

        def pooling(gi, pinb, sAb, sBb, dgb, N, kp, ks, kd, fix):
            E = 15 + N
            w = 2 << gi
            cur, oth = None, None
            bufs = [sAb, sBb]
            src = pinb
            sk = kp
            lo = 1
            m = 1
            lvl = 0
            while m < w:
                dst = bufs[lvl % 2]
                dk = ks[lvl % 2]
                sy.op("dve", lambda e, dst=dst, src=src, lo=lo, m=m: e.tensor_tensor(
                    out=dst[:, :, lo:E], in0=src[:, :, lo:E], in1=src[:, :, lo - m:E - m], op=ALU.add),
                    reads=sk, writes=dk)
                src, sk = dst, dk
                m *= 2
                lo = 2 * m - 1
                lvl += 1
            sy.op("dve", lambda e, src=src: e.scalar_tensor_tensor(out=dgb[:, :, :], in0=src[:, :, 15:E], scalar=1.0 / w,
                                                                   in1=pinb[:, :, 15:E], op0=ALU.mult, op1=ALU.subtract),
                  reads=sk + kp, writes=kd)
            if fix:
                for cc in range(2):
                    sy.op("dve", lambda e, cc=cc, src=src: e.tensor_tensor(out=src[:, cc, 15:15 + w - 1], in0=src[:, cc, 15:15 + w - 1],
                                                                           in1=self.rc[:, 0:w - 1], op=ALU.mult),
                          reads=sk + ["rc"], writes=sk)
                    sy.op("dve", lambda e, cc=cc, src=src: e.tensor_tensor(out=dgb[:, cc, 0:w - 1], in0=src[:, cc, 15:15 + w - 1],
                                                                           in1=pinb[:, cc, 15:15 + w - 1], op=ALU.subtract),
                          reads=sk + kp, writes=kd)

        for blk in (12, 13):
            wb, wk, _ = self.ring_get()
            wv = wb[:, :].rearrange("p (k n) -> p k n", n=512)
            for half in range(2):
                gi = (blk - 12) * 2 + half
                kp, ks, kd = self.pg(PIN, 4216), [self.pg(SA, 4216), self.pg(SB_, 4216)], self.pg(DG, 2048)

                def evac_p(c, b, gi=gi, kp=kp):
                    cc = c - 2 * (gi % 2)
                    cg = 2 * gi + cc
                    copy_op("act", pin[:, cc, 15:527], self.psb[b][:, :], [("ps", b)], kp)
                    sy.op("dve", lambda e: e.tensor_copy(out=pin[:, cc, 0:15], in_=self.ph[0][:, cg, :]), reads=["ph0"], writes=kp)
                    sy.op("dve", lambda e: e.tensor_copy(out=self.ph[0][:, cg, :], in_=pin[:, cc, 512:527]), reads=kp, writes=["ph0"])
                for c in (2 * half, 2 * half + 1):
                    b = self.next_bank()
                    self.mm_group(b, [(self.psb[b][:, :], wv[:, k, c * 128:(c + 1) * 128], hT[:, k, :], wk + hTk(k)) for k in range(KC)])
                    evac_p(c, b)
                pooling(gi, pin, sA, sB, dg, TT, kp, ks, kd, fix=(t == 0))
                if samp:
                    kps, kss, kds = ["tmps_pin"], [["tmps_sA"], ["tmps_sB"]], ["tmps_dg"]
                    for c in (2 * half, 2 * half + 1):
                        cc = c - 2 * half
                        cg = 2 * gi + cc
                        b = self.next_bank()
                        self.mm_group(b, [(self.psb[b][:, 0:NS], wv[:, k, c * 128:(c + 1) * 128], self.hT_s[:, k, :], wk + ["hTs"]) for k in range(KC)])
                        copy_op("act", pin_s[:, cc, 15:23], self.psb[b][:, 0:NS], [("ps", b)], kps)
                        sy.op("dve", lambda e, cc=cc, cg=cg: e.tensor_copy(out=pin_s[:, cc, 0:15], in_=self.ph[1][:, cg, :]), reads=["ph1"], writes=kps)
                        sy.op("dve", lambda e, cc=cc, cg=cg: e.tensor_copy(out=self.ph[1][:, cg, :], in_=pin_s[:, cc, 8:23]), reads=kps, writes=["ph1"])
                    pooling(gi, pin_s, sA_s, sB_s, dg_s, NS, kps, kss, kds, fix=False)
                stash = self.av(YC + 2048 * gi, 2048, BF16, "p (c t) -> p c t", t=TT)
                sy.op("dve", lambda e, stash=stash: e.tensor_copy(out=stash[:, :, :], in_=dg[:, :, :]),
                      reads=kd, writes=self.pg(YC + 2048 * gi, 2048))
                if samp:
                    st_s = self.t_st[gi][:]
                    sy.op("dve", lambda e, st_s=st_s: e.tensor_copy(out=st_s[:, :, :], in_=dg_s[:, :, :]),
                          reads=["tmps_dg"], writes=[("tmps_st", gi)])
        pwb, pwk, _ = self.ring_get()
        pw = pwb[:, 0:2048].rearrange("p (g kc d) -> p g kc d", g=4, kc=2)
        for g2 in range(4):
            dsrc = self.av(YC + 2048 * g2, 2048, BF16, "p (c t) -> p c t", t=TT)
            dkeys = self.pg(YC + 2048 * g2, 2048)
            for m in range(2):
                b = self.next_bank()
                self.mm_group(b, [(self.psb[b][:, :], pw[:, g2, kc, m * 128:(m + 1) * 128], dsrc[:, kc, :], pwk + dkeys) for kc in range(2)])
                ch = 8 + 2 * g2 + m
                sy.op("act", lambda e, b=b, ch=ch, g2=g2, m=m: e.activation(out=ycat[:, ch, :], in_=self.psb[b][:, :], func=AF.Copy,
                                                                            scale=self.psc_sb[:, l, 2 * g2 + m:2 * g2 + m + 1]),
                      reads=[("ps", b), "psc"], writes=self.pg(YC + 1024 * ch, 1024))
                if samp:
                    b = self.next_bank()
                    self.mm_group(b, [(self.psb[b][:, 0:NS], pw[:, g2, kc, m * 128:(m + 1) * 128], self.t_st[g2][:, kc, :], pwk + [("tmps_st", g2)]) for kc in range(2)])
                    sy.op("act", lambda e, b=b, ch=ch, g2=g2, m=m: e.activation(out=self.ycat_s[:, ch, :], in_=self.psb[b][:, 0:NS], func=AF.Copy,
                                                                                scale=self.psc_sb[:, l, 2 * g2 + m:2 * g2 + m + 1]),
                          reads=[("ps", b), "psc"], writes=["ycats"])

        if l == 0 and t == 0:
            self.dbg("dg3", dg, [P, 2, TT], BF16, self.pg(DG, 2048))
            self.dbg("pw", pw, [P, 4, 2, 256], BF16, pwk)
            self.dbg("pin3", pin, [P, 2, 527], F32, self.pg(PIN, 4216))
            self.dbg("sB3", sB, [P, 2, 527], F32, self.pg(SB_, 4216))
        self.stop("pool")
        GC, GB, UE, CU = TMP, TMP + 8192, TMP + 12288, TMP + 14352
        gc_sb = self.av(GC, 8192, F32, "p (c t) -> p c t", t=TT)
        gb_sb = self.av(GB, 4096, BF16, "p (c t) -> p c t", t=TT)
        ue = self.av(UE, 2056, F32)
        cu = self.av(CU, 2048, F32)
        gcs = self.t_gc[:]
        gbs = self.t_gb[:]
        ues = self.t_ue[:]
        cus = self.t_cu[:]

        def conv3(eng_h, out, ext, N, wsel, reads, writes):
            sy.op("dve", lambda e: e.tensor_scalar(out=out, in0=ext[:, 0:N], scalar1=wsel(0), scalar2=None, op0=ALU.mult),
                  reads=reads, writes=writes)
            sy.op("dve", lambda e: e.scalar_tensor_tensor(out=out, in0=ext[:, 1:N + 1], scalar=wsel(1), in1=out, op0=ALU.mult, op1=ALU.add),
                  reads=reads + writes, writes=writes)
            sy.op("dve", lambda e: e.scalar_tensor_tensor(out=out, in0=ext[:, 2:N + 2], scalar=wsel(2), in1=out, op0=ALU.mult, op1=ALU.add),
                  reads=reads + writes, writes=writes)

        for blk in (1, 2, 0):
            wb, wk, _ = self.ring_get()
            wv = wb[:, :].rearrange("p (k n) -> p k n", n=512)
            for c in range(4):
                b = self.next_bank()
                self.mm_group(b, [(self.psb[b][:, :], wv[:, k, c * 128:(c + 1) * 128], hT[:, k, :], wk + hTk(k)) for k in range(KC)])
                if samp:
                    b2 = self.next_bank()
                    self.mm_group(b2, [(self.psb[b2][:, 0:NS], wv[:, k, c * 128:(c + 1) * 128], self.hT_s[:, k, :], wk + ["hTs"]) for k in range(KC)])
                if blk == 1:
                    copy_op("act", gb_sb[:, c, :], self.psb[b][:, :], [("ps", b)], self.pg(GB + 1024 * c, 1024))
                    if samp:
                        copy_op("act", gbs[:, c, :], self.psb[b2][:, 0:NS], [("ps", b2)], ["tmps_gb"])
                elif blk == 2:
                    copy_op(ev_eng(), gc_sb[:, c, :], self.psb[b][:, :], [("ps", b)], self.pg(GC + 2048 * c, 2048))
                    if samp:
                        copy_op("act", gcs[:, c, :], self.psb[b2][:, 0:NS], [("ps", b2)], ["tmps_gc"])
                else:
                    for (pb, N, uext, cub, gcb, gbb, hist, yc, kgc, kgb, kue, kcu, kh, ky) in (
                        [(b, TT, ue, cu, gc_sb, gb_sb, self.uh[0], ycat, self.pg(GC + 2048 * c, 2048), self.pg(GB + 1024 * c, 1024),
                          self.pg(UE, 2056), self.pg(CU, 2048), ["uh0"], self.pg(YC + 1024 * c, 1024))] +
                        ([(b2, NS, ues, cus, gcs, gbs, self.uh[1], self.ycat_s, ["tmps_gc"], ["tmps_gb"], ["tmps_ue"], ["tmps_cu"], ["uh1"], ["ycats"])] if samp else [])):
                        sy.op("dve", lambda e, pb=pb, N=N, uext=uext, gcb=gcb: e.tensor_tensor(out=uext[:, 2:N + 2], in0=self.psb[pb][:, 0:N], in1=gcb[:, c, :], op=ALU.mult),
                              reads=[("ps", pb)] + kgc, writes=kue)
                        sy.op("dve", lambda e, uext=uext, hist=hist: e.tensor_copy(out=uext[:, 0:2], in_=hist[:, c, :]), reads=kh, writes=kue)
                        sy.op("dve", lambda e, N=N, uext=uext, hist=hist: e.tensor_copy(out=hist[:, c, :], in_=uext[:, N:N + 2]), reads=kue, writes=kh)
                        conv3(None, cub[:, 0:N], uext, N, lambda j: self.caw_sb[:, l, j, c:c + 1], kue + ["caw"], kcu)
                        sy.op("dve", lambda e, N=N, cub=cub, gbb=gbb, yc=yc: e.tensor_tensor(out=yc[:, c, :], in0=cub[:, 0:N], in1=gbb[:, c, :], op=ALU.mult),
                              reads=kcu + kgb, writes=ky)

        self.stop("mixA")
        def m16_view(ap512):
            return ap512.rearrange("p (q r) -> p r q", r=4)

        kvst_keys = [self.pg(KVST + 2048 * i, 2048) for i in range(4)]
        p_local = t * TT

        def out_kv(need_out, g, blkidx):
            if g == 0:
                return self.kv128_p[l, 0:128, :]
            if g == 1:
                return self.kv512_p[l, blkidx * 128:(blkidx + 1) * 128, :]
            return self.kv2048_p[l, t * TT + blkidx * 128:t * TT + (blkidx + 1) * 128, :]

        for g in range(3):
            wb, wk, _ = self.ring_get()
            wv = wb[:, :].rearrange("p (k n) -> p k n", n=512)
            for c in range(4):
                b = self.next_bank()
                self.mm_group(b, [(self.psb[b][:, :], wv[:, k, c * 128:(c + 1) * 128], hT[:, k, :], wk + hTk(k)) for k in range(KC)])
                src = self.psb[b][:, :] if g == 0 else m16_view(self.psb[b][:, :])
                dst = qT[:, g * 4 + c, :] if g == 0 else qT[:, g * 4 + c, :].rearrange("p (r q) -> p r q", r=4)
                copy_op(ev_eng(), dst, src, [("ps", b)], self.pg(QT + 1024 * (g * 4 + c), 1024))
                if samp:
                    b2 = self.next_bank()
                    self.mm_group(b2, [(self.psb[b2][:, 0:NS], wv[:, k, c * 128:(c + 1) * 128], self.hT_s[:, k, :], wk + ["hTs"]) for k in range(KC)])
                    copy_op("act", self.qT_s[:, g * 4 + c, :], self.psb[b2][:, 0:NS], [("ps", b2)], ["qTs"])
            wb, wk, _ = self.ring_get()
            wv = wb[:, :].rearrange("p (k n) -> p k n", n=512)
            for c in range(4):
                b = self.next_bank()
                self.mm_group(b, [(self.psb[b][:, :], wv[:, k, c * 128:(c + 1) * 128], hT[:, k, :], wk + hTk(k)) for k in range(KC)])
                if g == 0:
                    copy_op(ev_eng(), self.kT1[:, c, 128:640], self.psb[b][:, :], [("ps", b)], [("kT1", c)])
                elif g == 1:
                    copy_op(ev_eng(), self.kT2[:, c, par, :].rearrange("p (r q) -> p r q", r=4), m16_view(self.psb[b][:, :]),
                            [("ps", b)], [("kT2", c, par)])
                else:
                    copy_op(ev_eng(), self.kT3[:, c, :].rearrange("p (r q) -> p r q", r=4), m16_view(self.psb[b][:, :]),
                            [("ps", b)], [("kT3", c)])
                if samp:
                    b2 = self.next_bank()
                    self.mm_group(b2, [(self.psb[b2][:, 0:NS], wv[:, k, c * 128:(c + 1) * 128], self.hT_s[:, k, :], wk + ["hTs"]) for k in range(KC)])
                    copy_op("act", self.kT_s[:, g * 4 + c, :], self.psb[b2][:, 0:NS], [("ps", b2)], ["kTs"])
            need_out = [j for j in range(4) if (g == 2) or (t == 3 and (g == 1 or j == 3))]
            lhs_of = lambda j: ((lambda k: hT[:, k, j * 128:(j + 1) * 128]) if g == 0 else
                                (lambda k: hT[:, k, :].rearrange("p (q r) -> p r q", r=4)[:, j]))
            for j in need_out:
                si = self.kvst_i % 4
                self.kvst_i += 1
                b = self.next_bank()
                lhs = lhs_of(j)
                self.mm_group(b, [(self.psb[b][:, :], lhs(k), wv[:, k, :], wk + hTk(k)) for k in range(KC)])
                copy_op(ev_eng(), kvst[si][:, :], self.psb[b][:, :], [("ps", b)], kvst_keys[si])
                sy.dma("sp", out_kv(True, g, j)[:, 0:512], kvst[si][:, :], self.ds(f"okv{si}", True), reads=kvst_keys[si])
            if samp:
                si = self.kvst_i % 4
                self.kvst_i += 1
                b2 = self.next_bank()
                self.mm_group(b2, [(self.psb[b2][0:NS, :], self.hT_s[:, k, :], wv[:, k, :], wk + ["hTs"]) for k in range(KC)])
                copy_op("act", kvst[si][0:NS, :], self.psb[b2][0:NS, :], [("ps", b2)], kvst_keys[si])
                dst = (self.kv128_s[l, 120:128, 0:512], self.kv512_s[l, 504:512, 0:512], self.kv2048_s[l, 2040:2048, 0:512])[g]
                sy.dma("sp", dst, kvst[si][0:NS, :], self.ds(f"okv{si}", True), reads=kvst_keys[si])
            wb, wk, _ = self.ring_get()
            wv = wb[:, :].rearrange("p (k n) -> p k n", n=512)
            for j in range(4):
                lhs = lhs_of(j)
                b = self.next_bank()
                self.mm_group(b, [(self.psb[b][:, :], lhs(k), wv[:, k, :], wk + hTk(k)) for k in range(KC)])
                if g == 0:
                    copy_op("act", self.V1[:, 1 + j, :], self.psb[b][:, :], [("ps", b)], [("V1", 1 + j)])
                elif g == 1:
                    copy_op("act", self.V2[:, par, j, :], self.psb[b][:, :], [("ps", b)], [("V2", par, j)])
                else:
                    copy_op("act", self.V3[:, j, :], self.psb[b][:, :], [("ps", b)], [("V3", j)])
                if j in need_out:
                    si = self.kvst_i % 4
                    self.kvst_i += 1
                    copy_op("dve", kvst[si][:, :], self.psb[b][:, :], [("ps", b)], kvst_keys[si])
                    sy.dma("sp", out_kv(True, g, j)[:, 512:1024], kvst[si][:, :], self.ds(f"okv{si}", True), reads=kvst_keys[si])
            if samp:
                si = self.kvst_i % 4
                self.kvst_i += 1
                b2 = self.next_bank()
                self.mm_group(b2, [(self.psb[b2][0:NS, :], self.hT_s[:, k, :], wv[:, k, :], wk + ["hTs"]) for k in range(KC)])
                copy_op("act", kvst[si][0:NS, :], self.psb[b2][0:NS, :], [("ps", b2)], kvst_keys[si])
                copy_op("dve", self.V_s[0:NS, g, :], self.psb[b2][0:NS, :], [("ps", b2)], [("Vs", g)])
                dst = (self.kv128_s[l, 120:128, 512:1024], self.kv512_s[l, 504:512, 512:1024], self.kv2048_s[l, 2040:2048, 512:1024])[g]
                sy.dma("sp", dst, kvst[si][0:NS, :], self.ds(f"okv{si}", True), reads=kvst_keys[si])
        if t < 3:
            sy.dma("sp", self.hsc[l, t, :, 0:2048], self.kT3[:].rearrange("p h t -> p (h t)"), self.ds("hsc0", True),
                   reads=[("kT3", c) for c in range(4)], writes=[("hsc", l, t, 0)])
            sy.dma("sp", self.hsc[l, t, :, 2048:4096], self.V3[:].rearrange("p j t -> p (j t)"), self.ds("hsc1", True),
                   reads=[("V3", j) for j in range(4)], writes=[("hsc", l, t, 1)])

        if l == 0 and t == 0:
            self.dbg("qT", qT, [P, 12, TT], BF16, self.pg(QT, 12288))
            self.dbg("kT1", self.kT1[:], [P, 4, 640], BF16, [("kT1", c) for c in range(4)])
            self.dbg("kT3", self.kT3[:], [P, 4, 512], BF16, [("kT3", c) for c in range(4)])
            self.dbg("V1", self.V1[:], [P, 5, 512], BF16, [("V1", c) for c in range(5)])
            self.dbg("V3", self.V3[:], [P, 4, 512], BF16, [("V3", c) for c in range(4)])
            self.dbg("ycat", ycat, [P, KC, TT], BF16, self.pg(YC, 16384))

        self.stop("proj")
        hist = {}
        if t >= 1:
            grps = [(0, 1), (2,)] if t == 3 else [tuple(range(t))]
            got = self.ring_get(len(grps))
            if len(grps) == 1:
                got = [got]
            for grp, (hb, hk, _) in zip(grps, got):
                for j, tp in enumerate(grp):
                    hist[tp] = (hb[:, j * 4096:(j + 1) * 4096], hk)
        if samp:
            (cKb, cKk, _), (cVb, cVk, _) = self.ring_get(2)
        LT, OT, RL = KVST, KVST + 2048, KVST + 4096
        Lt = self.av(LT, 2048, F32)
        Ot = self.av(OT, 2048, F32)
        rL = self.av(RL, 2048, F32)
        SB = [0, 1, 2, 3]
        O_NAT, O_M16, L_NAT, L_M16 = 4, 5, 6, 7
        sbank = [0]
        scale = float(128 ** -0.5)

        head_tiles = {}

        def score_phase(h):
            ptb = ptbs[h % 2]
            PTh = PT + 8192 * (h % 2)
            tiles = []
            for half in range(2):
                subs = []
                for qi in range(2):
                    qb = half * 2 + qi
                    qap = qT[:, 0 * 4 + h, qb * 128:(qb + 1) * 128]
                    subs.append((self.kT1[:, h, qb * 128:(qb + 1) * 128], [("kT1", h)], qap, self.V1[:, qb, h * 128:(h + 1) * 128], [("V1", qb)],
                                 h * 2 + 1, "nat", qb * 128, not (t == 0 and qb == 0)))
                    subs.append((self.kT1[:, h, 128 + qb * 128:128 + (qb + 1) * 128], [("kT1", h)], qap, self.V1[:, 1 + qb, h * 128:(h + 1) * 128], [("V1", 1 + qb)],
                                 h * 2 + 0, "nat", qb * 128, True))
                tiles.append(subs)
            for kind in (0, 1):
                pp = par if kind == 0 else 1 - par
                subs = []
                for r in range(4):
                    qap = qT[:, 4 + h, r * 128:(r + 1) * 128]
                    subs.append((self.kT2[:, h, pp, r * 128:(r + 1) * 128], [("kT2", h, pp)], qap, self.V2[:, pp, r, h * 128:(h + 1) * 128], [("V2", pp, r)],
                                 8 + h * 2 + kind, "m16", r * 128, not (kind == 1 and t == 0)))
                tiles.append(subs)
            for tp in range(t, -1, -1):
                subs = []
                for r in range(4):
                    qap = qT[:, 8 + h, r * 128:(r + 1) * 128]
                    if tp == t:
                        kap, kk = self.kT3[:, h, r * 128:(r + 1) * 128], [("kT3", h)]
                        vap, vk = self.V3[:, r, h * 128:(h + 1) * 128], [("V3", r)]
                    else:
                        hb, hk = hist[tp]
                        kap, kk = hb[:, h * 512 + r * 128:h * 512 + (r + 1) * 128], hk
                        vap, vk = hb[:, 2048 + r * 512 + h * 128:2048 + r * 512 + (h + 1) * 128], hk
                    subs.append((kap, kk, qap, vap, vk, 16 + h * 4 + (t - tp), "m16", r * 128, True))
                tiles.append(subs)
            for ti, subs in enumerate(tiles):
                b = SB[sbank[0] % 4]
                sbank[0] += 1
                key = ("ps", b)
                sy.deps("pe", (), [key])
                for si, (kap, kk, qap, vap, vk, mi, tgt, col, valid) in enumerate(subs):
                    last = si == 3
                    qkey = self.pg(QT, 12288)
                    sy.op("pe", lambda e, b=b, si=si, kap=kap, qap=qap: e.matmul(self.psb[b][:, si * 128:(si + 1) * 128], kap, qap, start=True, stop=True),
                          reads=kk + qkey, writes=[key] if last else (), inc=last)
                pkeys = self.pg(PTh + 1024 * ti, 1024)
                sy.op("act", lambda e, b=b, ti=ti: e.activation(out=ptb[:, ti, :], in_=self.psb[b][:, :], func=AF.Exp, scale=scale),
                      reads=[key], writes=pkeys)
                for si, (kap, kk, qap, vap, vk, mi, tgt, col, valid) in enumerate(subs):
                    sy.op("dve", lambda e, ti=ti, si=si, mi=mi: e.tensor_tensor(out=ptb[:, ti, si * 128:(si + 1) * 128], in0=ptb[:, ti, si * 128:(si + 1) * 128],
                                                                                in1=self.masks[:, mi, :], op=ALU.mult),
                          reads=pkeys + ["masks"], writes=pkeys)
            head_tiles[h] = tiles

        for h in range(4):
            score_phase(h)
            tiles = head_tiles[h]
            ptb = ptbs[h % 2]
            PTh = PT + 8192 * (h % 2)
            started = {"nat": False, "m16": False}
            all_subs = [(ti, si, s) for ti, subs in enumerate(tiles) for si, s in enumerate(subs) if s[8]]
            cnt = {"nat": sum(1 for x in all_subs if x[2][6] == "nat"), "m16": sum(1 for x in all_subs if x[2][6] == "m16")}
            seen = {"nat": 0, "m16": 0}
            for ti, si, (kap, kk, qap, vap, vk, mi, tgt, col, valid) in all_subs:
                ob, lb = (O_NAT, L_NAT) if tgt == "nat" else (O_M16, L_M16)
                seen[tgt] += 1
                last = seen[tgt] == cnt[tgt]
                st = not started[tgt]
                started[tgt] = True
                pkeys = self.pg(PTh + 1024 * ti, 1024)
                pap = ptb[:, ti, si * 128:(si + 1) * 128]
                self.mm_raw(ob, self.psb[ob][:, col:col + 128], vap, pap, vk + pkeys, st, last)
                self.mm_raw(lb, self.psb[lb][:, col:col + 128], self.ones[:], pap, ["ones"] + pkeys, st, last)
            kl, ko, kr = self.pg(LT, 2048), self.pg(OT, 2048), self.pg(RL, 2048)
            sy.op("dve", lambda e: e.tensor_copy(out=Lt[:, :].rearrange("p (q r) -> p q r", r=4),
                                                 in_=self.psb[L_M16][:, :].rearrange("p (r q) -> p q r", r=4)),
                  reads=[("ps", L_M16)], writes=kl)
            sy.op("dve", lambda e: e.tensor_tensor(out=Lt[:, :], in0=self.psb[L_NAT][:, :], in1=Lt[:, :], op=ALU.add),
                  reads=[("ps", L_NAT)] + kl, writes=kl)
            sy.op("dve", lambda e: e.reciprocal(out=rL[:, :], in_=Lt[:, :]), reads=kl, writes=kr)
            sy.op("act", lambda e: e.activation(out=Ot[:, :].rearrange("p (q r) -> p q r", r=4),
                                                in_=self.psb[O_M16][:, :].rearrange("p (r q) -> p q r", r=4), func=AF.Copy),
                  reads=[("ps", O_M16)], writes=ko)
            sy.op("dve", lambda e: e.tensor_tensor(out=Ot[:, :], in0=self.psb[O_NAT][:, :], in1=Ot[:, :], op=ALU.add),
                  reads=[("ps", O_NAT)] + ko, writes=ko)
            sy.op("dve", lambda e, h=h: e.tensor_tensor(out=ycat[:, 4 + h, :], in0=Ot[:, :], in1=rL[:, :], op=ALU.mult),
                  reads=ko + kr, writes=self.pg(YC + 1024 * (4 + h), 1024))

            if samp:
                cK = cKb[:, 0:6656].rearrange("p (h r) -> p h r", h=4)
                cV = cVb[:, 0:6656].rearrange("p (b c) -> p b c", c=512)
                b = SB[sbank[0] % 4]
                sbank[0] += 1
                key = ("ps", b)
                sy.deps("pe", (), [key])
                jobs = [(cK[:, h, 0:128], self.qT_s[:, h, 0:8], 0, 8)]
                for r in range(4):
                    jobs.append((cK[:, h, 128 + r * 128:128 + (r + 1) * 128], self.qT_s[:, 4 + h, :].rearrange("p (j r) -> p r j", r=4)[:, r, :], 8 + 2 * r, 2))
                for n in range(8):
                    jobs.append((cK[:, h, 640 + n * 128:640 + (n + 1) * 128], self.qT_s[:, 8 + h, n:n + 1], 16 + n, 1))
                for ji, (kap, qap, c0, w) in enumerate(jobs):
                    sy.op("pe", lambda e, b=b, kap=kap, qap=qap, c0=c0, w=w: e.matmul(self.psb[b][:, c0:c0 + w], kap, qap, start=True, stop=True),
                          reads=cKk + ["qTs"], writes=(), inc=False)
                for g in range(3):
                    lastj = g == 2
                    sy.op("pe", lambda e, b=b, g=g: e.matmul(self.psb[b][0:NS, 24 + 8 * g:32 + 8 * g], self.kT_s[:, g * 4 + h, :], self.qT_s[:, g * 4 + h, :],
                                                               start=True, stop=True),
                          reads=["kTs", "qTs"], writes=[key] if lastj else (), inc=lastj)
                sy.op("act", lambda e, b=b, h=h: e.activation(out=self.pts[:, h, 0:24], in_=self.psb[b][:, 0:24], func=AF.Exp, scale=scale),
                      reads=[key], writes=[("pts", h)])
                sy.op("act", lambda e, b=b, h=h: e.activation(out=self.pts[0:NS, h, 24:48], in_=self.psb[b][0:NS, 24:48], func=AF.Exp, scale=scale),
                      reads=[key], writes=[("pts", h)])
                sy.op("dve", lambda e, h=h: e.tensor_tensor(out=self.pts[:, h, 0:24], in0=self.pts[:, h, 0:24], in1=self.smc[:, h, :], op=ALU.mult),
                      reads=[("pts", h), "smc"], writes=[("pts", h)])
                sy.op("dve", lambda e, h=h: e.tensor_tensor(out=self.pts[0:NS, h, 24:48], in0=self.pts[0:NS, h, 24:48], in1=self.smn[0:NS, h, :], op=ALU.mult),
                      reads=[("pts", h), "smn"], writes=[("pts", h)])
                ob, lb = O_NAT, L_NAT
                pv = [(cV[:, 0, h * 128:(h + 1) * 128], 128, self.pts[:, h, 0:8], slice(0, 8), cVk)]
                for r in range(4):
                    pv.append((cV[:, 1 + r, h * 128:(h + 1) * 128], 128, self.pts[:, h, 8 + 2 * r:10 + 2 * r], ("r", r), cVk))
                for n in range(8):
                    pv.append((cV[:, 5 + n, h * 128:(h + 1) * 128], 128, self.pts[:, h, 16 + n:17 + n], slice(n, n + 1), cVk))
                for g in range(3):
                    pv.append((self.V_s[0:NS, g, h * 128:(h + 1) * 128], NS, self.pts[0:NS, h, 24 + 8 * g:32 + 8 * g], slice(0, 8), [("Vs", g)]))
                for pi, (vap, kp_, pap, osel, vk) in enumerate(pv):
                    last = pi == len(pv) - 1
                    for bank, lhs, lk in ((ob, vap, vk), (lb, self.ones[0:kp_, :], ["ones"])):
                        if isinstance(osel, tuple):
                            oap = self.psb[bank][:, 0:8].rearrange("p (j r) -> p r j", r=4)[:, osel[1], :]
                        else:
                            oap = self.psb[bank][:, osel]
                        self.mm_raw(bank, oap, lhs, pap, lk + [("pts", h)], pi == 0, last)
                sy.op("dve", lambda e: e.reciprocal(out=self.t_rl[:], in_=self.psb[lb][:, 0:8]), reads=[("ps", lb)], writes=["tmps_rl"])
                sy.op("dve", lambda e, h=h: e.tensor_tensor(out=self.ycat_s[:, 4 + h, :], in0=self.psb[ob][:, 0:8], in1=self.t_rl[:], op=ALU.mult),
                      reads=[("ps", ob), "tmps_rl"], writes=["ycats"])

        for h in range(4):
            sy.op("dve", lambda e, h=h: e.tensor_copy(out=self.kT1[:, h, 0:128], in_=self.kT1[:, h, 512:640]), reads=[("kT1", h)], writes=[("kT1", h)])
        sy.op("dve", lambda e: e.tensor_copy(out=self.V1[:, 0, :], in_=self.V1[:, 4, :]), reads=[("V1", 4)], writes=[("V1", 0)])

        self.stop("attn")
        self.norm_gbc_load(l, 1)
        for n in range(4):
            wb, wk, _ = self.ring_get()
            wv = wb[:, :].rearrange("p (k n) -> p k n", n=512)
            for j in range(4):
                b = self.next_bank()
                self.mm_group(b, [(self.psb[b][:, :], ycat[:, k, j * 128:(j + 1) * 128], wv[:, k, :], wk + self.pg(YC + 1024 * k, 1024)) for k in range(KC)])
                sy.op("dve", lambda e, b=b, j=j, n=n: e.tensor_tensor(out=self.x[:, j, n * 512:(n + 1) * 512], in0=self.psb[b][:, :],
                                                                      in1=self.x[:, j, n * 512:(n + 1) * 512], op=ALU.add),
                      reads=[("ps", b), ("x", j)], writes=[("x", j)])
            if samp:
                for c in range(4):
                    b = self.next_bank()
                    self.mm_group(b, [(self.psb[b][:, 0:NS], wv[:, k, c * 128:(c + 1) * 128], self.ycat_s[:, k, :], wk + ["ycats"]) for k in range(KC)])
                    sy.op("dve", lambda e, b=b, c=c, n=n: e.tensor_tensor(out=self.xs[:, 4 * n + c, :], in0=self.psb[b][:, 0:NS],
                                                                          in1=self.xs[:, 4 * n + c, :], op=ALU.add),
                          reads=[("ps", b), "xs"], writes=["xs"])

        self.stop("out")
        hT = self.norm_tm(l, 1)
        if samp:
            rs = self.norm_sample(None)
            for k in range(KC):
                sy.op("dve", lambda e, k=k: e.tensor_tensor(out=self.t_t[:], in0=self.xs[:, k, :], in1=rs, op=ALU.mult),
                      reads=["xs", "tmps_rs"], writes=["tmps_t"])
                sy.op("dve", lambda e, k=k: e.tensor_scalar(out=self.hT_s[:, k, :], in0=self.t_t[:],
                                                            scalar1=self.g12_sb[:, l, 1, k:k + 1], scalar2=None, op0=ALU.mult),
                      reads=["tmps_t", "g12"], writes=["hTs"])
        if l == 1:
            sy.dma("sp", self.av(53248, 8192, F32)[:, :], self.gfin_bc, self.ds("ldf"), writes=self.pg(53248, 8192))
        ACT0, GT, CV = 16384, 16384 + 22528, 16384 + 22528 + 4128
        actb = [self.av(ACT0 + 11264 * i, 11264, BF16, "p (c t) -> p c t", t=TT) for i in range(2)]
        gt = [self.av(GT + 2064 * i, 2056, F32) for i in range(2)]
        gts = self.t_gt[:]
        cvs = self.t_cv[:]
        cnt_ch = [0]

        cvb = [self.av(CV + 2048 * i, 2048, F32) for i in range(4)]
        cvs4 = [self.t_cv4[i][:] for i in range(4)]

        def gate_up(gi):
            c0, c1 = FGROUPS[gi]
            c = c0
            while c < c1:
                ce = min(c + 4, c1)
                nc_ = ce - c
                variants = [(TT, lambda k: hT[:, k, :], hTk, self.gh[0], "gh0")]
                if samp:
                    variants.append((NS, lambda k: self.hT_s[:, k, :], lambda k: ["hTs"], self.gh[1], "gh1"))
                wgb, wgk, _ = self.ring_get()
                wg = wgb[:, 0:KC * nc_ * 128].rearrange("p (k n) -> p k n", n=nc_ * 128)
                for cc in range(nc_):
                    cg = c + cc
                    for vi, (N, rhs, rk, hist_, hk_) in enumerate(variants):
                        if vi == 0:
                            gtb, kgt = gt[cnt_ch[0] % 2], self.pg(GT + 2064 * (cnt_ch[0] % 2), 2056)
                            cb, kcv = cvb[cc], self.pg(CV + 2048 * cc, 2048)
                            cnt_ch[0] += 1
                        else:
                            gtb, kgt = gts, ["tmps_gt"]
                            cb, kcv = cvs4[cc], [("tmps_cv", cc)]
                        bg = self.next_bank()
                        self.mm_group(bg, [(self.psb[bg][:, 0:N], wg[:, k, cc * 128:(cc + 1) * 128], rhs(k), wgk + rk(k)) for k in range(KC)])
                        copy_op("act", gtb[:, 2:N + 2], self.psb[bg][:, 0:N], [("ps", bg)], kgt)
                        sy.op("dve", lambda e: e.tensor_copy(out=gtb[:, 0:2], in_=hist_[:, cg, :]), reads=[hk_], writes=kgt)
                        sy.op("dve", lambda e: e.tensor_copy(out=hist_[:, cg, :], in_=gtb[:, N:N + 2]), reads=kgt, writes=[hk_])
                        sy.op("act", lambda e: e.activation(out=cb[:, 0:N], in_=gtb[:, 0:N], func=AF.Copy, scale=self.fcw_sb[:, l, 0, cg:cg + 1]),
                              reads=kgt + ["fcw"], writes=kcv)
                        sy.op("dve", lambda e: e.scalar_tensor_tensor(out=cb[:, 0:N], in0=gtb[:, 1:N + 1], scalar=self.fcw_sb[:, l, 1, cg:cg + 1],
                                                                      in1=cb[:, 0:N], op0=ALU.mult, op1=ALU.add),
                              reads=kgt + kcv + ["fcw"], writes=kcv)
                        sy.op("dve", lambda e: e.scalar_tensor_tensor(out=cb[:, 0:N], in0=gtb[:, 2:N + 2], scalar=self.fcw_sb[:, l, 2, cg:cg + 1],
                                                                      in1=cb[:, 0:N], op0=ALU.mult, op1=ALU.add),
                              reads=kgt + kcv + ["fcw"], writes=kcv)
                        sy.op("act", lambda e: e.activation(out=cb[:, 0:N], in_=cb[:, 0:N], func=AF.Silu), reads=kcv, writes=kcv)
                wub, wuk, _ = self.ring_get()
                wu = wub[:, 0:KC * nc_ * 128].rearrange("p (k n) -> p k n", n=nc_ * 128)
                for cc in range(nc_):
                    cg = c + cc
                    ci = cg - c0
                    for vi, (N, rhs, rk, hist_, hk_) in enumerate(variants):
                        if vi == 0:
                            cb, kcv = cvb[cc], self.pg(CV + 2048 * cc, 2048)
                            aout, kao = actb[gi % 2][:, ci, :], self.pg(ACT0 + 11264 * (gi % 2) + 1024 * ci, 1024)
                        else:
                            cb, kcv = cvs4[cc], [("tmps_cv", cc)]
                            aout, kao = self.act_s[:, cg, :], ["acts"]
                        bu = self.next_bank()
                        self.mm_group(bu, [(self.psb[bu][:, 0:N], wu[:, k, cc * 128:(cc + 1) * 128], rhs(k), wuk + rk(k)) for k in range(KC)])
                        sy.op("dve", lambda e: e.tensor_tensor(out=aout, in0=self.psb[bu][:, 0:N], in1=cb[:, 0:N], op=ALU.mult),
                              reads=[("ps", bu)] + kcv, writes=kao)
                c = ce

        def down(gi):
            c0, c1 = FGROUPS[gi]
            nch = c1 - c0
            for n in range(4):
                wb, wk, _ = self.ring_get()
                wv = wb[:, 0:nch * 512].rearrange("p (k n) -> p k n", n=512)
                for j in range(4):
                    b = self.next_bank()
                    self.mm_group(b, [(self.psb[b][:, :], actb[gi % 2][:, ci, j * 128:(j + 1) * 128], wv[:, ci, :],
                                       wk + self.pg(ACT0 + 11264 * (gi % 2) + 1024 * ci, 1024)) for ci in range(nch)])
                    sy.op("dve", lambda e, b=b, j=j, n=n: e.tensor_tensor(out=self.x[:, j, n * 512:(n + 1) * 512], in0=self.psb[b][:, :],
                                                                          in1=self.x[:, j, n * 512:(n + 1) * 512], op=ALU.add),
                          reads=[("ps", b), ("x", j)], writes=[("x", j)])
                if samp:
                    for m in range(4):
                        b = self.next_bank()
                        self.mm_group(b, [(self.psb[b][:, 0:NS], wv[:, ci, m * 128:(m + 1) * 128], self.act_s[:, c0 + ci, :], wk + ["acts"]) for ci in range(nch)])
                        sy.op("dve", lambda e, b=b, m=m, n=n: e.tensor_tensor(out=self.xs[:, 4 * n + m, :], in0=self.psb[b][:, 0:NS],
                                                                              in1=self.xs[:, 4 * n + m, :], op=ALU.add),
                              reads=[("ps", b), "xs"], writes=["xs"])

        gate_up(0); gate_up(1); down(0); gate_up(2); down(1); gate_up(3); down(2); down(3)

        self.stop("ffn")
        if l == 0:
            for j in range(4):
                sy.dma("sp", self.xsc[t, :, j * D:(j + 1) * D], self.x[:, j, :], self.ds(f"osc{j}", True), reads=[("x", j)], writes=[("xsc", t, j)])
        else:
            GF, JK, YS = 53248, 61440, 16384
            gf = self.av(GF, 8192, F32)
            junk = self.av(JK, 4096, BF16)
            yst = [self.av(YS + 8192 * i, 8192, F32) for i in range(3)]
            for j in range(4):
                ysk = self.pg(YS + 8192 * (j % 3), 8192)
                sy.op("act", lambda e, j=j: e.activation(out=junk[:, :], in_=self.x[:, j, :], func=AF.Square,
                                                         accum_out=self.stat[:, j:j + 1]),
                      reads=[("x", j)], writes=self.pg(JK, 4096) + [("stat", j)])
                sy.op("act", lambda e, j=j: e.activation(out=self.stat[:, 4 + j:5 + j], in_=self.stat[:, j:j + 1], func=AF.Sqrt,
                                                         bias=self.epsb[:, 0:1], scale=1.0 / D),
                      reads=[("stat", j), "epsb"], writes=[("stat", 4 + j)])
                sy.op("dve", lambda e, j=j: e.reciprocal(out=self.stat[:, 8 + j:9 + j], in_=self.stat[:, 4 + j:5 + j]),
                      reads=[("stat", 4 + j)], writes=[("stat", 8 + j)])
                sy.op("dve", lambda e, j=j: e.scalar_tensor_tensor(out=yst[j % 3][:, :], in0=self.x[:, j, :], scalar=self.stat[:, 8 + j:9 + j],
                                                                   in1=gf[:, :], op0=ALU.mult, op1=ALU.mult),
                      reads=[("x", j), ("stat", 8 + j)] + self.pg(GF, 8192), writes=ysk)
                sy.dma("sp", self.y_p[t * TT + j * P:t * TT + (j + 1) * P, :], yst[j % 3][:, :], self.ds(f"oy{j % 3}", True), reads=ysk)
            if samp:
                rs = self.norm_sample(None)
                yo = self.t_yo[:]
                for k in range(KC):
                    sy.op("dve", lambda e, k=k: e.tensor_tensor(out=self.t_t[:], in0=self.xs[:, k, :], in1=rs, op=ALU.mult),
                          reads=["xs", "tmps_rs"], writes=["tmps_t"])
                    sy.op("dve", lambda e, k=k: e.tensor_scalar(out=yo[:, k, :], in0=self.t_t[:],
                                                                scalar1=self.gfin_sb[:, k:k + 1], scalar2=None, op0=ALU.mult),
                          reads=["tmps_t", "gfin"], writes=["tmps_yo"])
                sy.dma("sp", self.y_s, yo, self.ds("oys", True), reads=["tmps_yo"])


_CACHE = {}


def _get_nc():
    if "nc" not in _CACHE:
        b = Builder()
        _CACHE["nc"] = b.build()
    return _CACHE["nc"]


def _fm(v, nchunk):
    return v


def prepare_inputs(inputs, core):
    f = lambda a: np.ascontiguousarray(np.asarray(a, dtype=np.float32))
    b, s = core // 2, core
    m = {}
    m["xp"] = f(inputs["x_prompt"][b])
    m["xs_fm"] = f(np.asarray(inputs["x_sample"][s]).reshape(NS, KC, P).transpose(2, 1, 0))
    for k in ("w_in", "w_out", "w_gate", "w_up", "w_down", "pool_w"):
        m[k] = f(inputs[k])
    n1 = np.asarray(inputs["norm1"]).reshape(DEPTH, KC, P)
    n2 = np.asarray(inputs["norm2"]).reshape(DEPTH, KC, P)
    m["g12"] = f(np.stack([n1, n2], axis=1).transpose(3, 0, 1, 2))
    fn = np.asarray(inputs["final_norm"])
    m["gfin_bc"] = f(np.broadcast_to(fn[None, :], (P, D)))
    g2 = np.stack([np.asarray(inputs["norm1"]), np.asarray(inputs["norm2"])], axis=1)
    m["gbc12"] = f(np.broadcast_to(g2[:, :, None, :], (DEPTH, 2, P, D)))
    m["gfin_fm"] = f(fn.reshape(KC, P).T)
    m["caw"] = f(np.asarray(inputs["conv_a_w"]).reshape(DEPTH, 3, 4, P).transpose(3, 0, 1, 2))
    m["psc"] = f(np.asarray(inputs["pool_scale"]).reshape(DEPTH, 8, P).transpose(2, 0, 1))
    m["fcw"] = f(np.asarray(inputs["ffn_conv_w"]).reshape(DEPTH, 3, FC, P).transpose(3, 0, 1, 2))
    m["s_conv"] = f(np.asarray(inputs["state_conv_a"])[:, s].reshape(DEPTH, 2, 4, P).transpose(0, 3, 2, 1))
    m["s_pool"] = f(np.asarray(inputs["state_pool"])[:, s].reshape(DEPTH, 15, 8, P).transpose(0, 3, 2, 1))
    m["s_ffn"] = f(np.asarray(inputs["state_ffn_conv"])[:, s].reshape(DEPTH, 2, FC, P).transpose(0, 3, 2, 1))
    c128 = np.asarray(inputs["cache_kv_w128"])[:, s]
    c512 = np.asarray(inputs["cache_kv_w512"])[:, s]
    c2048 = np.asarray(inputs["cache_kv_w2048"])[:, s]
    m["c128"] = f(c128.reshape(DEPTH, 128, 1024))
    m["c512"] = f(c512.reshape(DEPTH, 512, 1024))
    m["c2048"] = f(c2048.reshape(DEPTH, 2048, 1024))
    rows = [c128]
    for r in range(4):
        rows.append(c512[:, r::4])
    for n in range(8):
        rows.append(c2048[:, n::16])
    blk = np.stack(rows, axis=1)
    kk = blk[:, :, :, 0]
    m["ckT"] = f(kk.transpose(0, 4, 3, 1, 2).reshape(DEPTH, P, 4 * 1664))
    vv = blk[:, :, :, 1]
    m["cV"] = f(vv.transpose(0, 2, 1, 3, 4).reshape(DEPTH, P, 13 * 512))
    M, SC, SN = build_masks()
    m["masks"] = f(M.reshape(P, 32 * 128))
    m["smask_c"] = f(SC.reshape(P, 96))
    m["smask_n"] = f(SN.reshape(P, 96))
    rc = np.zeros((P, 16), np.float32)
    rc[:, :15] = 1.0 / np.arange(1, 16, dtype=np.float32)
    m["rc_tab"] = rc
    m["ident"] = np.eye(P, dtype=np.float32)
    return m


def assemble(results, ncores=8):
    perm = m16_of_natural()
    y_p = np.zeros((4, SEQ, D), np.float32)
    y_s = np.zeros((8, NS, D), np.float32)
    kv128_p = np.zeros((DEPTH, 4, 128, 2, 4, 128), np.float32)
    kv512_p = np.zeros((DEPTH, 4, 512, 2, 4, 128), np.float32)
    kv2048_p = np.zeros((DEPTH, 4, 2048, 2, 4, 128), np.float32)
    kv128_s = np.zeros((DEPTH, 8, 128, 2, 4, 128), np.float32)
    kv512_s = np.zeros((DEPTH, 8, 512, 2, 4, 128), np.float32)
    kv2048_s = np.zeros((DEPTH, 8, 2048, 2, 4, 128), np.float32)
    conv_p = np.zeros((DEPTH, 4, 2, 512), np.float32)
    conv_s = np.zeros((DEPTH, 8, 2, 512), np.float32)
    pool_p = np.zeros((DEPTH, 4, 15, 1024), np.float32)
    pool_s = np.zeros((DEPTH, 8, 15, 1024), np.float32)
    ffn_p = np.zeros((DEPTH, 4, 2, DFF), np.float32)
    ffn_s = np.zeros((DEPTH, 8, 2, DFF), np.float32)
    for c in range(ncores):
        r = results[c]
        s = c
        y_s[s] = np.asarray(r["y_s"]).transpose(2, 1, 0).reshape(NS, D)
        kv128_s[:, s] = np.asarray(r["kv128_s"]).reshape(DEPTH, 128, 2, 4, 128)
        kv512_s[:, s] = np.asarray(r["kv512_s"]).reshape(DEPTH, 512, 2, 4, 128)
        kv2048_s[:, s] = np.asarray(r["kv2048_s"]).reshape(DEPTH, 2048, 2, 4, 128)
        conv_s[:, s] = np.asarray(r["conv_s"]).transpose(0, 3, 2, 1).reshape(DEPTH, 2, 512)
        pool_s[:, s] = np.asarray(r["pool_s"]).transpose(0, 3, 2, 1).reshape(DEPTH, 15, 1024)
        ffn_s[:, s] = np.asarray(r["ffn_s"]).transpose(0, 3, 2, 1).reshape(DEPTH, 2, DFF)
        if c % 2 == 0:
            b = c // 2
            y_p[b] = np.asarray(r["y_p"])
            kv128_p[:, b] = np.asarray(r["kv128_p"]).reshape(DEPTH, 128, 2, 4, 128)
            k5 = np.asarray(r["kv512_p"]).reshape(DEPTH, 512, 2, 4, 128)
            tmp = np.zeros_like(k5)
            tmp[:, perm] = k5
            kv512_p[:, b] = tmp
            k2 = np.asarray(r["kv2048_p"]).reshape(DEPTH, 4, 512, 2, 4, 128)
            tmp = np.zeros_like(k2)
            tmp[:, :, perm] = k2
            kv2048_p[:, b] = tmp.reshape(DEPTH, 2048, 2, 4, 128)
            conv_p[:, b] = np.asarray(r["conv_p"]).transpose(0, 3, 2, 1).reshape(DEPTH, 2, 512)
            pool_p[:, b] = np.asarray(r["pool_p"]).transpose(0, 3, 2, 1).reshape(DEPTH, 15, 1024)
            ffn_p[:, b] = np.asarray(r["ffn_p"]).transpose(0, 3, 2, 1).reshape(DEPTH, 2, DFF)
    return (y_p, y_s, kv128_p, kv128_s, kv512_p, kv512_s, kv2048_p, kv2048_s,
            conv_p, conv_s, pool_p, pool_s, ffn_p, ffn_s)


def kernel(**inputs):
    nc = _get_nc()
    in_maps = [prepare_inputs(inputs, c) for c in range(8)]
    res = run_bass_kernel_spmd(nc, in_maps, core_ids=list(range(8)))
    return assemble(res.results)
```

```python
import numpy as np
import concourse.bass as bass
import concourse.mybir as mybir
from concourse.bass_utils import run_bass_kernel_spmd

F32 = mybir.dt.float32
BF16 = mybir.dt.bfloat16
ALU = mybir.AluOpType
AF = mybir.ActivationFunctionType

P = 128
D = 2048
KC = 16
TT = 512
NT = 4
NS = 8
DFF = 5504
FC = 43
IN_COLS = 7168
SEQ = 2048
DEPTH = 2
EPS = 1e-6
FGROUPS = [(0, 11), (11, 22), (22, 33), (33, 43)]


class S:
    def __init__(self, nc, name, owner):
        self.h = nc.semaphore(name).__enter__()
        self.name = name
        self.owner = owner
        self.count = 0


class Sync:
    def __init__(self, nc):
        self.nc = nc
        self.eng = {}
        for name, h in (("pe", nc.tensor), ("act", nc.scalar), ("dve", nc.vector),
                        ("pool", nc.gpsimd), ("sp", nc.sync)):
            self.eng[name] = dict(h=h, sem=None, known={})
        self.last_w = {}
        self.readers = {}
        self.nsem = 0

    def sem(self, name, owner):
        self.nsem += 1
        return S(self.nc, name, owner)

    def new_epoch(self, tag):
        for n in ("pe", "act", "dve"):
            self.eng[n]["sem"] = self.sem(f"{n}_{tag}", n)

    def wait(self, engname, stamp, raw):
        if stamp is None:
            return
        s, v = stamp
        if s.owner == engname and (engname == "pe" or not raw):
            return
        e = self.eng[engname]
        if e["known"].get(s.name, 0) >= v:
            return
        e["h"].wait_ge(s.h, v)
        e["known"][s.name] = v

    def deps(self, engname, reads, writes):
        for k in reads:
            self.wait(engname, self.last_w.get(k), True)
            if isinstance(k, tuple) and k[0] == "ps":
                for st in self.readers.get(k, {}).values():
                    self.wait(engname, st, False)
        for k in writes:
            self.wait(engname, self.last_w.get(k), False)
            for st in self.readers.get(k, {}).values():
                self.wait(engname, st, False)

    def record(self, stamp, reads, writes):
        for k in reads:
            self.readers.setdefault(k, {})[stamp[0].name] = stamp
        for k in writes:
            self.last_w[k] = stamp
            self.readers[k] = {}

    def op(self, engname, fn, reads=(), writes=(), inc=True):
        self.deps(engname, reads, writes)
        e = self.eng[engname]
        ins = fn(e["h"])
        s = e["sem"]
        if inc:
            s.count += 1
            ins.then_inc(s.h, 1)
            stamp = (s, s.count)
        else:
            stamp = (s, s.count + 1)
        self.record(stamp, reads, writes)
        return ins

    def dma(self, qname, out, in_, dsem, reads=(), writes=()):
        self.deps(qname, reads, writes)
        ins = self.eng[qname]["h"].dma_start(out=out, in_=in_)
        dsem.count += 16
        ins.then_inc(dsem.h, 16)
        self.record((dsem, dsem.count), reads, writes)


def m16_of_natural():
    perm = np.zeros(512, np.int64)
    for r in range(4):
        for q in range(128):
            perm[r * 128 + q] = 4 * q + r
    return perm


def build_masks():
    h12 = np.arange(1, 13, dtype=np.float32)
    slopes = np.power(np.float32(2.0), -8.0 * h12 / 12).astype(np.float64)
    M = np.zeros((128, 32, 128), np.float64)
    ik = np.arange(128)[:, None]
    iq = np.arange(128)[None, :]
    for h in range(4):
        s1 = slopes[h]
        d = iq - ik
        M[:, h * 2 + 0, :] = np.where(d >= 0, np.exp(-s1 * d), 0.0)
        d = iq - ik + 128
        M[:, h * 2 + 1, :] = np.where(d <= 128, np.exp(-s1 * d), 0.0)
        s2 = slopes[4 + h]
        sk = ik
        sq = iq
        d = sq - sk
        M[:, 8 + h * 2 + 0, :] = np.where(d >= 0, np.exp(-s2 * 4.0 * d), 0.0)
        d = sq - sk + 128
        M[:, 8 + h * 2 + 1, :] = np.where(d <= 128, np.exp(-s2 * 4.0 * d), 0.0)
        s3 = slopes[8 + h]
        same = (ik % 4) == (iq % 4)
        for dt_ in range(4):
            d = (iq // 4) - (ik // 4) + 32 * dt_
            M[:, 16 + h * 4 + dt_, :] = np.where(same & (d >= 0), np.exp(-s3 * 16.0 * d), 0.0)
    SC = np.zeros((128, 4, 24), np.float64)
    SN = np.zeros((128, 4, 24), np.float64)
    for h in range(4):
        s1, s2, s3 = slopes[h], slopes[4 + h], slopes[8 + h]
        for c in range(128):
            for n in range(8):
                d = 128 + n - c
                if d <= 128:
                    SC[c, h, n] = np.exp(-s1 * d)
            for r in range(4):
                for j in range(2):
                    d = 512 + 4 * j - 4 * c
                    if c >= j:
                        SC[c, h, 8 + 2 * r + j] = np.exp(-s2 * d)
            for n in range(8):
                SC[c, h, 16 + n] = np.exp(-s3 * (2048 - 16 * c))
        for n2 in range(8):
            for n in range(8):
                if n2 <= n:
                    SN[n2, h, n] = np.exp(-s1 * (n - n2))
                    if (n - n2) % 4 == 0:
                        SN[n2, h, 8 + n] = np.exp(-s2 * (n - n2))
                if n2 == n:
                    SN[n2, h, 16 + n] = 1.0
    return M.astype(np.float32), SC.astype(np.float32), SN.astype(np.float32)


class StopBuild(Exception):
    pass


STOP_AFTER = None


DEBUG = False


class Builder:
    def dbg(self, name, ap, shape, dt, keys):
        if not DEBUG:
            return
        d = self.nc.dram_tensor("dbg_" + name, list(shape), dt, kind="ExternalOutput").ap()
        self.sy.dma("sp", d, ap, self.ds("dbg_" + name, True), reads=keys)

    def ds(self, name, out=False):
        if name not in self.dsem:
            self.dsem[name] = self.sy.sem(name, "dma")
            if out:
                self.out_sems.append(self.dsem[name])
        return self.dsem[name]

    def stop(self, name):
        if STOP_AFTER == name:
            raise StopBuild(name)

    def __init__(self):
        nc = bass.Bass("TRN2", target_bir_lowering=False)
        self.nc = nc
        self.sy = Sync(nc)
        self.declare_dram()
        self.alloc()
        self.bank_rr = 0

    def din(self, name, shape, dt=F32):
        return self.nc.dram_tensor(name, list(shape), dt, kind="ExternalInput").ap()

    def dout(self, name, shape, dt=F32):
        return self.nc.dram_tensor(name, list(shape), dt, kind="ExternalOutput").ap()

    def declare_dram(self):
        self.xp = self.din("xp", [SEQ, D])
        self.xs_in = self.din("xs_fm", [P, KC, NS])
        self.w_in = self.din("w_in", [DEPTH, D, IN_COLS])
        self.w_out = self.din("w_out", [DEPTH, D, D])
        self.w_gate = self.din("w_gate", [DEPTH, D, DFF])
        self.w_up = self.din("w_up", [DEPTH, D, DFF])
        self.w_down = self.din("w_down", [DEPTH, DFF, D])
        self.pool_w = self.din("pool_w", [DEPTH, 4, 256, 256])
        self.g12 = self.din("g12", [P, DEPTH, 2, KC])
        self.gfin_bc = self.din("gfin_bc", [P, D])
        self.gbc12 = self.din("gbc12", [DEPTH, 2, P, D])
        self.gfin_fm = self.din("gfin_fm", [P, KC])
        self.caw = self.din("caw", [P, DEPTH, 3, 4])
        self.psc = self.din("psc", [P, DEPTH, 8])
        self.fcw = self.din("fcw", [P, DEPTH, 3, FC])
        self.s_conv = self.din("s_conv", [DEPTH, P, 4, 2])
        self.s_pool = self.din("s_pool", [DEPTH, P, 8, 15])
        self.s_ffn = self.din("s_ffn", [DEPTH, P, FC, 2])
        self.ckT = self.din("ckT", [DEPTH, P, 4 * 1664])
        self.cV = self.din("cV", [DEPTH, P, 13 * 512])
        self.c128 = self.din("c128", [DEPTH, 128, 1024])
        self.c512 = self.din("c512", [DEPTH, 512, 1024])
        self.c2048 = self.din("c2048", [DEPTH, 2048, 1024])
        self.masks_d = self.din("masks", [P, 32 * 128])
        self.smc_d = self.din("smask_c", [P, 4 * 24])
        self.smn_d = self.din("smask_n", [P, 4 * 24])
        self.rc_d = self.din("rc_tab", [P, 16])
        self.ident_d = self.din("ident", [P, P])
        self.y_p = self.dout("y_p", [SEQ, D])
        self.y_s = self.dout("y_s", [P, KC, NS])
        self.kv128_p = self.dout("kv128_p", [DEPTH, 128, 1024])
        self.kv128_s = self.dout("kv128_s", [DEPTH, 128, 1024])
        self.kv512_p = self.dout("kv512_p", [DEPTH, 512, 1024])
        self.kv512_s = self.dout("kv512_s", [DEPTH, 512, 1024])
        self.kv2048_p = self.dout("kv2048_p", [DEPTH, 2048, 1024])
        self.kv2048_s = self.dout("kv2048_s", [DEPTH, 2048, 1024])
        self.conv_p = self.dout("conv_p", [DEPTH, P, 4, 2])
        self.conv_s = self.dout("conv_s", [DEPTH, P, 4, 2])
        self.pool_p = self.dout("pool_p", [DEPTH, P, 8, 15])
        self.pool_s = self.dout("pool_s", [DEPTH, P, 8, 15])
        self.ffn_p = self.dout("ffn_p", [DEPTH, P, FC, 2])
        self.ffn_s = self.dout("ffn_s", [DEPTH, P, FC, 2])
        self.xsc = self.nc.dram_tensor("xsc", [NT, P, 4 * D], F32, kind="Internal").ap()
        self.hsc = self.nc.dram_tensor("hsc", [DEPTH, NT, P, 4096], BF16, kind="Internal").ap()

    def sb(self, name, shape, dt):
        return self.nc.alloc_sbuf_tensor("sb_" + name, list(shape), dt)

    def alloc(self):
        nc = self.nc
        self.x = self.sb("x", [P, 4, D], F32)
        self.kT1 = self.sb("kT1", [P, 4, 640], BF16)
        self.kT2 = self.sb("kT2", [P, 4, 2, 512], BF16)
        self.kT3 = self.sb("kT3", [P, 4, 512], BF16)
        self.V1 = self.sb("V1", [P, 5, 512], BF16)
        self.V2 = self.sb("V2", [P, 2, 4, 512], BF16)
        self.V3 = self.sb("V3", [P, 4, 512], BF16)
        self.ring = [self.sb(f"ring{i}", [P, 8192], BF16) for i in range(3)]
        self.masks = self.sb("masks", [P, 32, 128], BF16)
        self.smc = self.sb("smc", [P, 4, 24], BF16)
        self.smn = self.sb("smn", [P, 4, 24], BF16)
        self.rc = self.sb("rc", [P, 16], F32)
        self.g12_sb = self.sb("g12s", [P, DEPTH, 2, KC], F32)
        self.gfin_sb = self.sb("gfins", [P, KC], F32)
        self.caw_sb = self.sb("caws", [P, DEPTH, 3, 4], F32)
        self.psc_sb = self.sb("pscs", [P, DEPTH, 8], F32)
        self.fcw_sb = self.sb("fcws", [P, DEPTH, 3, FC], F32)
        self.uh = [self.sb(f"uh{i}", [P, 4, 2], F32) for i in range(2)]
        self.ph = [self.sb(f"ph{i}", [P, 8, 15], F32) for i in range(2)]
        self.gh = [self.sb(f"gh{i}", [P, FC, 2], F32) for i in range(2)]
        self.ident = self.sb("ident", [P, P], BF16)
        self.ones = self.sb("ones", [P, P], BF16)
        self.xs = self.sb("xs", [P, KC, NS], F32)
        self.stat = self.sb("stat", [P, 16], F32)
        self.epsb = self.sb("epsb", [P, 1], F32)
        self.hT_s = self.sb("hTs", [P, KC, NS], BF16)
        self.ycat_s = self.sb("ycats", [P, KC, NS], BF16)
        self.qT_s = self.sb("qTs", [P, 12, NS], BF16)
        self.kT_s = self.sb("kTs", [P, 12, NS], BF16)
        self.V_s = self.sb("Vs", [P, 3, 512], BF16)
        self.act_s = self.sb("acts", [P, FC, NS], BF16)
        self.pts = self.sb("pts", [P, 4, 48], BF16)
        self.t_sq = self.sb("t_sq", [P, KC, NS], F32)
        self.t_sqb = self.sb("t_sqb", [P, KC, NS], BF16)
        self.t_lo = self.sb("t_lo", [P, KC, NS], F32)
        self.t_lob = self.sb("t_lob", [P, KC, NS], BF16)
        self.t_rs = self.sb("t_rs", [P, NS], F32)
        self.t_t = self.sb("t_t", [P, NS], F32)
        self.t_yo = self.sb("t_yo", [P, KC, NS], F32)
        self.t_pin = self.sb("t_pin", [P, 2, 23], F32)
        self.t_sA = self.sb("t_sA", [P, 2, 23], F32)
        self.t_sB = self.sb("t_sB", [P, 2, 23], F32)
        self.t_dg = self.sb("t_dg", [P, 2, NS], BF16)
        self.t_st = [self.sb(f"t_st{i}", [P, 2, NS], BF16) for i in range(4)]
        self.t_gc = self.sb("t_gc", [P, 4, NS], F32)
        self.t_gb = self.sb("t_gb", [P, 4, NS], F32)
        self.t_ue = self.sb("t_ue", [P, NS + 2], F32)
        self.t_cu = self.sb("t_cu", [P, NS], F32)
        self.t_gt = self.sb("t_gt", [P, NS + 2], F32)
        self.t_cv = self.sb("t_cv", [P, NS], F32)
        self.t_cv4 = [self.sb(f"t_cv4{i}", [P, NS], F32) for i in range(4)]
        self.t_rl = self.sb("t_rl", [P, NS], F32)
        self.ARENA = 71680
        self.arena = self.sb("arena", [P, self.ARENA // 4], F32)
        self.psb = [nc.alloc_psum_tensor(f"ps{i}", [P, 512], F32) for i in range(8)]

    def av(self, off, nbytes, dt, pattern=None, **kw):
        assert off % 4 == 0 and nbytes % 4 == 0 and off + nbytes <= self.ARENA, (off, nbytes)
        ap = self.arena[:, off // 4:(off + nbytes) // 4]
        if dt == BF16:
            ap = ap.bitcast(BF16)
        if pattern:
            ap = ap.rearrange(pattern, **kw)
        return ap

    @staticmethod
    def pg(off, nbytes):
        return [("A", i) for i in range(off // 1024, (off + nbytes - 1) // 1024 + 1)]

    def next_bank(self):
        b = self.bank_rr
        self.bank_rr = (self.bank_rr + 1) % 8
        return b

    def mm_group(self, bank, mms, extra_w=()):
        sy = self.sy
        key = ("ps", bank)
        sy.deps("pe", (), [key])
        n = len(mms)
        for i, (o, l, r, rk) in enumerate(mms):
            last = i == n - 1
            sy.op("pe", lambda e, o=o, l=l, r=r, i=i, last=last: e.matmul(o, l, r, start=(i == 0), stop=last),
                  reads=rk, writes=([key] + list(extra_w)) if last else (), inc=last)

    def mm_raw(self, bank, o, l, r, rk, start, last_inc):
        sy = self.sy
        key = ("ps", bank)
        if start:
            sy.deps("pe", (), [key])
        sy.op("pe", lambda e: e.matmul(o, l, r, start=start, stop=False, skip_group_check=True),
              reads=rk, writes=[key] if last_inc else (), inc=last_inc)

    def ring_init(self):
        self.rq = []
        self.rq_emitted = 0
        self.rq_taken = 0
        self.rsem = [self.sy.sem(f"ring{i}", "dma") for i in range(3)]

    def ring_declare(self, descs):
        self.rq.extend(descs)

    def ring_emit_load(self, i):
        d = self.rq[i]
        slot = i % 3
        for (dst_lo, n, pattern, kw, src, skeys) in d["parts"]:
            dst = self.ring[slot][:, dst_lo:dst_lo + n]
            if pattern:
                dst = dst.rearrange(pattern, **kw)
            self.sy.dma("pool", dst, src, self.rsem[slot], reads=skeys, writes=[("ring", slot)])

    def ring_get(self, n=1):
        first = self.rq_taken
        assert n <= 2
        while self.rq_emitted < min(len(self.rq), first + 3):
            self.ring_emit_load(self.rq_emitted)
            self.rq_emitted += 1
        out = []
        for i in range(first, first + n):
            slot = i % 3
            out.append((self.ring[slot], [("ring", slot)], self.rq[i]))
        self.rq_taken += n
        return out[0] if n == 1 else out

    @staticmethod
    def wdesc(tag, src3, kc, ncols):
        return dict(tag=tag, kc=kc, ncols=ncols,
                    parts=[(0, kc * ncols, "p (k n) -> p k n", dict(n=ncols), src3, ())])

    def pass_blocks(self, l, t):
        d = []
        win = self.w_in[l].rearrange("(k p) n -> p k n", p=P)
        order = [12, 13, 1, 2, 0] + list(range(3, 12))
        for blk in order:
            d.append(self.wdesc(("in", blk), win[:, :, blk * 512:(blk + 1) * 512], KC, 512))
            if blk == 13:
                pw = self.pool_w[l].rearrange("g (kc p) d -> p g kc d", p=P)
                d.append(dict(tag=("pw",), kc=0, ncols=0,
                              parts=[(0, 2048, "p (g kc d) -> p g kc d", dict(g=4, kc=2), pw, ())]))
        if t >= 1:
            prs = [(0, 1)] if t == 3 else [tuple(range(t))]
            if t == 3:
                prs = [(0, 1), (2,)]
            for grp in prs:
                parts = []
                for j, tp in enumerate(grp):
                    parts.append((j * 4096, 4096, None, None, self.hsc[l, tp], [("hsc", l, tp, 0), ("hsc", l, tp, 1)]))
                d.append(dict(tag=("hist", grp), kc=0, ncols=0, parts=parts))
        if t == 0:
            d.append(dict(tag=("cK",), kc=0, ncols=0, parts=[(0, 6656, None, None, self.ckT[l], ())]))
            d.append(dict(tag=("cV",), kc=0, ncols=0, parts=[(0, 6656, None, None, self.cV[l], ())]))
        wo = self.w_out[l].rearrange("(k p) n -> p k n", p=P)
        for n in range(4):
            d.append(self.wdesc(("out", n), wo[:, :, n * 512:(n + 1) * 512], KC, 512))
        wg = self.w_gate[l].rearrange("(k p) n -> p k n", p=P)
        wu = self.w_up[l].rearrange("(k p) n -> p k n", p=P)
        wd = self.w_down[l].rearrange("(c p) n -> p c n", p=P)

        def gu(gi):
            c0, c1 = FGROUPS[gi]
            out = []
            c = c0
            while c < c1:
                ce = min(c + 4, c1)
                out.append(self.wdesc(("gate", c, ce), wg[:, :, c * 128:ce * 128], KC, (ce - c) * 128))
                out.append(self.wdesc(("up", c, ce), wu[:, :, c * 128:ce * 128], KC, (ce - c) * 128))
                c = ce
            return out

        def dn(gi):
            c0, c1 = FGROUPS[gi]
            return [self.wdesc(("down", gi, n), wd[:, c0:c1, n * 512:(n + 1) * 512], c1 - c0, 512) for n in range(4)]

        d += gu(0) + gu(1) + dn(0) + gu(2) + dn(1) + gu(3) + dn(2) + dn(3)
        return d

    def build(self):
        nc, sy = self.nc, self.sy
        self.ring_init()
        self.dsem = {}
        self.out_sems = []
        sy.new_epoch("init")
        self.prologue()
        for l in range(DEPTH):
            for t in range(NT):
                self.ring_declare(self.pass_blocks(l, t))
        try:
            self.stop("prologue")
            for l in range(DEPTH):
                self.layer_start(l)
                for t in range(NT):
                    sy.new_epoch(f"l{l}t{t}")
                    self.run_pass(l, t)
                    self.stop(f"pass{l}{t}")
                self.layer_end(l)
        except StopBuild:
            pass
        for sm in self.out_sems:
            if sm.count:
                nc.sync.wait_ge(sm.h, sm.count)
        return nc

    def prologue(self):
        sy, ds = self.sy, self.dsem
        sy.dma("pool", self.masks[:].rearrange("p t q -> p (t q)"), self.masks_d, self.ds("miscp"), writes=["masks"])
        sy.dma("pool", self.smc[:].rearrange("p h q -> p (h q)"), self.smc_d, self.ds("miscp"), writes=["smc"])
        sy.dma("pool", self.smn[:].rearrange("p h q -> p (h q)"), self.smn_d, self.ds("miscp"), writes=["smn"])
        sy.dma("pool", self.ident[:], self.ident_d, self.ds("miscp"), writes=["ident"])
        for dst, src, k in ((self.rc, self.rc_d, "rc"), (self.g12_sb, self.g12, "g12"), (self.gfin_sb, self.gfin_fm, "gfin"),
                            (self.caw_sb, self.caw, "caw"), (self.psc_sb, self.psc, "psc"), (self.fcw_sb, self.fcw, "fcw"),
                            (self.xs, self.xs_in, "xs")):
            sy.dma("sp", dst[:], src, self.ds("misc"), writes=[k])
        for k in ("masks", "smc", "smn", "ident"):
            sy.last_w[k] = (self.ds("miscp"), self.ds("miscp").count)
        for k in ("rc", "g12", "gfin", "caw", "psc", "fcw", "xs"):
            sy.last_w[k] = (self.ds("misc"), self.ds("misc").count)
        sy.op("dve", lambda e: e.memset(self.ones[:], 1.0), writes=["ones"])
        sy.op("dve", lambda e: e.memset(self.epsb[:], EPS), writes=["epsb"])
        for tname, ten in (("kT1", self.kT1), ("kT2", self.kT2), ("kT3", self.kT3), ("V1", self.V1), ("V2", self.V2), ("V3", self.V3)):
            sy.op("dve", lambda e, ten=ten: e.memset(ten[:], 0.0), writes=[tname])
        for l in range(DEPTH):
            sy.dma("sp", self.kv128_s[l, 0:120, :], self.c128[l, 8:128, :], self.ds("ocp", True))
            sy.dma("sp", self.kv512_s[l, 0:504, :], self.c512[l, 8:512, :], self.ds("ocp", True))
            sy.dma("sp", self.kv2048_s[l, 0:2040, :], self.c2048[l, 8:2048, :], self.ds("ocp", True))

    def layer_start(self, l):
        sy, ds = self.sy, self.dsem
        for ten, k in ((self.uh[0], "uh0"), (self.ph[0], "ph0"), (self.gh[0], "gh0")):
            sy.op("dve", lambda e, ten=ten: e.memset(ten[:], 0.0), writes=[k])
        sy.dma("sp", self.uh[1][:], self.s_conv[l], self.ds("lds0"), writes=["uh1"])
        sy.dma("sp", self.ph[1][:], self.s_pool[l], self.ds("lds1"), writes=["ph1"])
        sy.dma("sp", self.gh[1][:], self.s_ffn[l], self.ds("lds2"), writes=["gh1"])

    def layer_end(self, l):
        sy, ds = self.sy, self.dsem
        for src, dst, k in ((self.uh[0], self.conv_p[l], "uh0"), (self.ph[0], self.pool_p[l], "ph0"), (self.gh[0], self.ffn_p[l], "gh0"),
                            (self.uh[1], self.conv_s[l], "uh1"), (self.ph[1], self.pool_s[l], "ph1"), (self.gh[1], self.ffn_s[l], "gh1")):
            sy.dma("sp", dst, src[:], self.ds("ost_" + k, True), reads=[k])

    def norm_gbc_load(self, l, which):
        return

    def norm_tm(self, l, which):
        sy = self.sy
        R0, R1 = 0, 16384
        hT = self.av(R0, 16384, BF16, "p (k t) -> p k t", t=TT)
        htm = self.av(R1, 16384, BF16, "p (j d) -> p j d", d=D)
        for j in range(4):
            sy.op("act", lambda e, j=j: e.activation(out=htm[:, j, :], in_=self.x[:, j, :], func=AF.Square,
                                                     accum_out=self.stat[:, j:j + 1]),
                  reads=[("x", j)], writes=self.pg(R1 + 4096 * j, 4096) + [("stat", j)])
            sy.op("act", lambda e, j=j: e.activation(out=self.stat[:, 4 + j:5 + j], in_=self.stat[:, j:j + 1], func=AF.Sqrt,
                                                     bias=self.epsb[:, 0:1], scale=1.0 / D),
                  reads=[("stat", j), "epsb"], writes=[("stat", 4 + j)])
            sy.op("dve", lambda e, j=j: e.reciprocal(out=self.stat[:, 8 + j:9 + j], in_=self.stat[:, 4 + j:5 + j]),
                  reads=[("stat", 4 + j)], writes=[("stat", 8 + j)])
            sy.op("dve", lambda e, j=j: e.tensor_scalar(out=htm[:, j, :], in0=self.x[:, j, :], scalar1=self.stat[:, 8 + j:9 + j],
                                                        scalar2=None, op0=ALU.mult),
                  reads=[("x", j), ("stat", 8 + j)], writes=self.pg(R1 + 4096 * j, 4096))
        for k in range(KC):
            b = self.next_bank()
            pst = self.psb[b][:, 0:256].bitcast(BF16)
            key = ("ps", b)
            sy.deps("pe", (), [key])
            for j in range(4):
                last = j == 3
                sy.op("pe", lambda e, j=j, k=k, pst=pst: e.transpose(pst[:, j * 128:(j + 1) * 128], htm[:, j, k * 128:(k + 1) * 128], self.ident[:]),
                      reads=self.pg(R1 + 4096 * j, 4096) + ["ident"], writes=[key] if last else (), inc=last)
            sy.op("act", lambda e, k=k, pst=pst: e.activation(out=hT[:, k, :], in_=pst, func=AF.Copy,
                                                              scale=self.g12_sb[:, l, which, k:k + 1]),
                  reads=[key, "g12"], writes=self.pg(R0 + 1024 * k, 1024))
        return hT

    def norm_sample(self, gsel):
        sy = self.sy
        sq = self.t_sq[:]
        sy.op("dve", lambda e: e.tensor_tensor(out=sq, in0=self.xs[:], in1=self.xs[:], op=ALU.mult),
              reads=["xs"], writes=["tmps_sq"])
        sqb = self.t_sqb[:]
        sy.op("dve", lambda e: e.tensor_copy(out=sqb, in_=sq), reads=["tmps_sq"], writes=["tmps_sqb"])
        lo = self.t_lo[:]
        sy.op("dve", lambda e: e.tensor_tensor(out=lo, in0=sq, in1=sqb, op=ALU.subtract),
              reads=["tmps_sq", "tmps_sqb"], writes=["tmps_lo"])
        lob = self.t_lob[:]
        sy.op("dve", lambda e: e.tensor_copy(out=lob, in_=lo), reads=["tmps_lo"], writes=["tmps_lob"])
        b = self.next_bank()
        mms = []
        for src, kk in ((sqb, "tmps_sqb"), (lob, "tmps_lob")):
            for k in range(KC):
                mms.append((self.psb[b][:, 0:NS], self.ones[:], src[:, k, :], [kk, "ones"]))
        self.mm_group(b, mms)
        rs = self.t_rs[:]
        sy.op("act", lambda e: e.activation(out=self.t_t[:], in_=self.psb[b][:, 0:NS], func=AF.Sqrt, bias=self.epsb[:, 0:1], scale=1.0 / D),
              reads=[("ps", b), "epsb"], writes=["tmps_t"])
        sy.op("dve", lambda e: e.reciprocal(out=rs, in_=self.t_t[:]), reads=["tmps_t"], writes=["tmps_rs"])
        return rs

    def run_pass(self, l, t):
        sy, ds = self.sy, self.dsem
        samp = (t == 0)
        par = t % 2
        if l == 0:
            for j in range(4):
                sy.dma("sp", self.x[:, j, :], self.xp[t * TT + j * P:t * TT + (j + 1) * P, :], self.ds(f"ldx{j}"), writes=[("x", j)])
        else:
            for j in range(4):
                sy.dma("sp", self.x[:, j, :], self.xsc[t, :, j * D:(j + 1) * D], self.ds(f"ldx{j}"),
                       reads=[("xsc", t, j)], writes=[("x", j)])
        self.stop("load")
        self.norm_gbc_load(l, 0)
        hT = self.norm_tm(l, 0)
        if l == 0 and t == 0:
            self.dbg("stat", self.stat[:], [P, 16], F32, [("stat", i) for i in range(12)])
            self.dbg("hT", hT, [P, KC, TT], BF16, self.pg(0, 16384))
            self.dbg("htm", self.av(16384, 16384, BF16, "p (j d) -> p j d", d=D), [P, 4, D], BF16, self.pg(16384, 16384))
        self.stop("norm1")
        hTk = lambda k: self.pg(1024 * k, 1024)
        if samp:
            rs = self.norm_sample(None)
            for k in range(KC):
                sy.op("dve", lambda e, k=k: e.tensor_tensor(out=self.t_t[:], in0=self.xs[:, k, :], in1=rs, op=ALU.mult),
                      reads=["xs", "tmps_rs"], writes=["tmps_t"])
                sy.op("dve", lambda e, k=k: e.tensor_scalar(out=self.hT_s[:, k, :], in0=self.t_t[:],
                                                            scalar1=self.g12_sb[:, l, 0, k:k + 1], scalar2=None, op0=ALU.mult),
                      reads=["tmps_t", "g12"], writes=["hTs"])
        YC, QT, TMP, KVST = 16384, 32768, 45056, 62464
        PT = TMP
        ycat = self.av(YC, 16384, BF16, "p (k t) -> p k t", t=TT)
        qT = self.av(QT, 12288, BF16, "p (k t) -> p k t", t=TT)
        ptbs = [self.av(PT + 8192 * i, 8192, BF16, "p (k t) -> p k t", t=TT) for i in range(2)]
        kvst = [self.av(KVST + 2048 * i, 2048, F32) for i in range(4)]
        self.kvst_i = 0
        evt = [0]

        def ev_eng():
            evt[0] += 1
            return "act" if evt[0] % 2 else "dve"

        def copy_op(eng, out, in_, reads, writes):
            if eng == "act":
                sy.op("act", lambda e: e.activation(out=out, in_=in_, func=AF.Copy), reads=reads, writes=writes)
            else:
                sy.op("dve", lambda e: e.tensor_copy(out=out, in_=in_), reads=reads, writes=writes)

        def fm_block(wb, wk, nchunks, rhs_of_k, rkeys_of_k, N, evac):
            for c in range(nchunks):
                b = self.next_bank()
                mms = [(self.psb[b][:, 0:N], wb[:, k, c * 128:(c + 1) * 128], rhs_of_k(k), wk + rkeys_of_k(k)) for k in range(KC)]
                self.mm_group(b, mms)
                evac(c, b)

        hs_of_k = lambda k: self.hT_s[:, k, :]
        hs_keys = lambda k: ["hTs"]
        hp_of_k = lambda k: hT[:, k, :]

        PIN, SA, SB_, DG = TMP, TMP + 4224, TMP + 8448, TMP + 12672
        pin = self.av(PIN, 4216, F32, "p (c e) -> p c e", e=527)
        sA = self.av(SA, 4216, F32, "p (c e) -> p c e", e=527)
        sB = self.av(SB_, 4216, F32, "p (c e) -> p c e", e=527)
        dg = self.av(DG, 2048, BF16, "p (c t) -> p c t", t=TT)
        pin_s = self.t_pin[:]
        sA_s = self.t_sA[:]
        sB_s = self.t_sB[:]
        dg_s = self.t_dg[:]
        pending_pool = []

        def pooling(gi, pinb, sAb, sBb, dgb, N, kp, ks, kd, fix):
            E = 15 + N
            w = 2 << gi
            cur, oth = None, None
            bufs = [sAb, sBb]
            src = pinb
            sk = kp
            lo = 1
            m = 1
            lvl = 0
            while m < w:
                dst = bufs[lvl % 2]
                dk = ks[lvl % 2]
                sy.op("dve", lambda e, dst=dst, src=src, lo=lo, m=m: e.tensor_tensor(
                    out=dst[:, :, lo:E], in0=src[:, :, lo:E], in1=src[:, :, lo - m:E - m], op=ALU.add),
                    reads=sk, writes=dk)
                src, sk = dst, dk
                m *= 2
                lo = 2 * m - 1
                lvl += 1
            sy.op("dve", lambda e, src=src: e.scalar_tensor_tensor(out=dgb[:, :, :], in0=src[:, :, 15:E], scalar=1.0 / w,
                                                                   in1=pinb[:, :, 15:E], op0=ALU.mult, op1=ALU.subtract),
                  reads=sk + kp, writes=kd)
            if fix:
                for cc in range(2):
                    sy.op("dve", lambda e, cc=cc, src=src: e.tensor_tensor(out=src[:, cc, 15:15 + w - 1], in0=src[:, cc, 15:15 + w - 1],
                                                                           in1=self.rc[:, 0:w - 1], op=ALU.mult),
                          reads=sk + ["rc"], writes=sk)
                    sy.op("dve", lambda e, cc=cc, src=src: e.tensor_tensor(out=dgb[:, cc, 0:w - 1], in0=src[:, cc, 15:15 + w - 1],
                                                                           in1=pinb[:, cc, 15:15 + w - 1], op=ALU.subtract),
                          reads=sk + kp, writes=kd)

        for blk in (12, 13):
            wb, wk, _ = self.ring_get()
            wv = wb[:, :].rearrange("p (k n) -> p k n", n=512)
            for half in range(2):
                gi = (blk - 12) * 2 + half
                kp, ks, kd = self.pg(PIN, 4216), [self.pg(SA, 4216), self.pg(SB_, 4216)], self.pg(DG, 2048)

                def evac_p(c, b, gi=gi, kp=kp):
                    cc = c - 2 * (gi % 2)
                    cg = 2 * gi + cc
                    copy_op("act", pin[:, cc, 15:527], self.psb[b][:, :], [("ps", b)], kp)
                    sy.op("dve", lambda e: e.tensor_copy(out=pin[:, cc, 0:15], in_=self.ph[0][:, cg, :]), reads=["ph0"], writes=kp)
                    sy.op("dve", lambda e: e.tensor_copy(out=self.ph[0][:, cg, :], in_=pin[:, cc, 512:527]), reads=kp, writes=["ph0"])
                for c in (2 * half, 2 * half + 1):
                    b = self.next_bank()
                    self.mm_group(b, [(self.psb[b][:, :], wv[:, k, c * 128:(c + 1) * 128], hT[:, k, :], wk + hTk(k)) for k in range(KC)])
                    evac_p(c, b)
                pooling(gi, pin, sA, sB, dg, TT, kp, ks, kd, fix=(t == 0))
                if samp:
                    kps, kss, kds = ["tmps_pin"], [["tmps_sA"], ["tmps_sB"]], ["tmps_dg"]
                    for c in (2 * half, 2 * half + 1):
                        cc = c - 2 * half
                        cg = 2 * gi + cc
                        b = self.next_bank()
                        self.mm_group(b, [(self.psb[b][:, 0:NS], wv[:, k, c * 128:(c + 1) * 128], self.hT_s[:, k, :], wk + ["hTs"]) for k in range(KC)])
                        copy_op("act", pin_s[:, cc, 15:23], self.psb[b][:, 0:NS], [("ps", b)], kps)
                        sy.op("dve", lambda e, cc=cc, cg=cg: e.tensor_copy(out=pin_s[:, cc, 0:15], in_=self.ph[1][:, cg, :]), reads=["ph1"], writes=kps)
                        sy.op("dve", lambda e, cc=cc, cg=cg: e.tensor_copy(out=self.ph[1][:, cg, :], in_=pin_s[:, cc, 8:23]), reads=kps, writes=["ph1"])
                    pooling(gi, pin_s, sA_s, sB_s, dg_s, NS, kps, kss, kds, fix=False)
                stash = self.av(YC + 2048 * gi, 2048, BF16, "p (c t) -> p c t", t=TT)
                sy.op("dve", lambda e, stash=stash: e.tensor_copy(out=stash[:, :, :], in_=dg[:, :, :]),
                      reads=kd, writes=self.pg(YC + 2048 * gi, 2048))
                if samp:
                    st_s = self.t_st[gi][:]
                    sy.op("dve", lambda e, st_s=st_s: e.tensor_copy(out=st_s[:, :, :], in_=dg_s[:, :, :]),
                          reads=["tmps_dg"], writes=[("tmps_st", gi)])
        pwb, pwk, _ = self.ring_get()
        pw = pwb[:, 0:2048].rearrange("p (g kc d) -> p g kc d", g=4, kc=2)
        for g2 in range(4):
            dsrc = self.av(YC + 2048 * g2, 2048, BF16, "p (c t) -> p c t", t=TT)
            dkeys = self.pg(YC + 2048 * g2, 2048)
            for m in range(2):
                b = self.next_bank()
                self.mm_group(b, [(self.psb[b][:, :], pw[:, g2, kc, m * 128:(m + 1) * 128], dsrc[:, kc, :], pwk + dkeys) for kc in range(2)])
                ch = 8 + 2 * g2 + m
                sy.op("act", lambda e, b=b, ch=ch, g2=g2, m=m: e.activation(out=ycat[:, ch, :], in_=self.psb[b][:, :], func=AF.Copy,
                                                                            scale=self.psc_sb[:, l, 2 * g2 + m:2 * g2 + m + 1]),
                      reads=[("ps", b), "psc"], writes=self.pg(YC + 1024 * ch, 1024))
                if samp:
                    b = self.next_bank()
                    self.mm_group(b, [(self.psb[b][:, 0:NS], pw[:, g2, kc, m * 128:(m + 1) * 128], self.t_st[g2][:, kc, :], pwk + [("tmps_st", g2)]) for kc in range(2)])
                    sy.op("act", lambda e, b=b, ch=ch, g2=g2, m=m: e.activation(out=self.ycat_s[:, ch, :], in_=self.psb[b][:, 0:NS], func=AF.Copy,
                                                                                scale=self.psc_sb[:, l, 2 * g2 + m:2 * g2 + m + 1]),
                          reads=[("ps", b), "psc"], writes=["ycats"])

        if l == 0 and t == 0:
            self.dbg("dg3", dg, [P, 2, TT], BF16, self.pg(DG, 2048))
            self.dbg("pw", pw, [P, 4, 2, 256], BF16, pwk)
            self.dbg("pin3", pin, [P, 2, 527], F32, self.pg(PIN, 4216))
            self.dbg("sB3", sB, [P, 2, 527], F32, self.pg(SB_, 4216))
        self.stop("pool")
        GC, GB, UE, CU = TMP, TMP + 8192, TMP + 12288, TMP + 14352
        gc_sb = self.av(GC, 8192, F32, "p (c t) -> p c t", t=TT)
        gb_sb = self.av(GB, 4096, BF16, "p (c t) -> p c t", t=TT)
        ue = self.av(UE, 2056, F32)
        cu = self.av(CU, 2048, F32)
        gcs = self.t_gc[:]
        gbs = self.t_gb[:]
        ues = self.t_ue[:]
        cus = self.t_cu[:]

        def conv3(eng_h, out, ext, N, wsel, reads, writes):
            sy.op("dve", lambda e: e.tensor_scalar(out=out, in0=ext[:, 0:N], scalar1=wsel(0), scalar2=None, op0=ALU.mult),
                  reads=reads, writes=writes)
            sy.op("dve", lambda e: e.scalar_tensor_tensor(out=out, in0=ext[:, 1:N + 1], scalar=wsel(1), in1=out, op0=ALU.mult, op1=ALU.add),
                  reads=reads + writes, writes=writes)
            sy.op("dve", lambda e: e.scalar_tensor_tensor(out=out, in0=ext[:, 2:N + 2], scalar=wsel(2), in1=out, op0=ALU.mult, op1=ALU.add),
                  reads=reads + writes, writes=writes)

        for blk in (1, 2, 0):
            wb, wk, _ = self.ring_get()
            wv = wb[:, :].rearrange("p (k n) -> p k n", n=512)
            for c in range(4):
                b = self.next_bank()
                self.mm_group(b, [(self.psb[b][:, :], wv[:, k, c * 128:(c + 1) * 128], hT[:, k, :], wk + hTk(k)) for k in range(KC)])
                if samp:
                    b2 = self.next_bank()
                    self.mm_group(b2, [(self.psb[b2][:, 0:NS], wv[:, k, c * 128:(c + 1) * 128], self.hT_s[:, k, :], wk + ["hTs"]) for k in range(KC)])
                if blk == 1:
                    copy_op("act", gb_sb[:, c, :], self.psb[b][:, :], [("ps", b)], self.pg(GB + 1024 * c, 1024))
                    if samp:
                        copy_op("act", gbs[:, c, :], self.psb[b2][:, 0:NS], [("ps", b2)], ["tmps_gb"])
                elif blk == 2:
                    copy_op(ev_eng(), gc_sb[:, c, :], self.psb[b][:, :], [("ps", b)], self.pg(GC + 2048 * c, 2048))
                    if samp:
                        copy_op("act", gcs[:, c, :], self.psb[b2][:, 0:NS], [("ps", b2)], ["tmps_gc"])
                else:
                    for (pb, N, uext, cub, gcb, gbb, hist, yc, kgc, kgb, kue, kcu, kh, ky) in (
                        [(b, TT, ue, cu, gc_sb, gb_sb, self.uh[0], ycat, self.pg(GC + 2048 * c, 2048), self.pg(GB + 1024 * c, 1024),
                          self.pg(UE, 2056), self.pg(CU, 2048), ["uh0"], self.pg(YC + 1024 * c, 1024))] +
                        ([(b2, NS, ues, cus, gcs, gbs, self.uh[1], self.ycat_s, ["tmps_gc"], ["tmps_gb"], ["tmps_ue"], ["tmps_cu"], ["uh1"], ["ycats"])] if samp else [])):
                        sy.op("dve", lambda e, pb=pb, N=N, uext=uext, gcb=gcb: e.tensor_tensor(out=uext[:, 2:N + 2], in0=self.psb[pb][:, 0:N], in1=gcb[:, c, :], op=ALU.mult),
                              reads=[("ps", pb)] + kgc, writes=kue)
                        sy.op("dve", lambda e, uext=uext, hist=hist: e.tensor_copy(out=uext[:, 0:2], in_=hist[:, c, :]), reads=kh, writes=kue)
                        sy.op("dve", lambda e, N=N, uext=uext, hist=hist: e.tensor_copy(out=hist[:, c, :], in_=uext[:, N:N + 2]), reads=kue, writes=kh)
                        conv3(None, cub[:, 0:N], uext, N, lambda j: self.caw_sb[:, l, j, c:c + 1], kue + ["caw"], kcu)
                        sy.op("dve", lambda e, N=N, cub=cub, gbb=gbb, yc=yc: e.tensor_tensor(out=yc[:, c, :], in0=cub[:, 0:N], in1=gbb[:, c, :], op=ALU.mult),
                              reads=kcu + kgb, writes=ky)

        self.stop("mixA")
        def m16_view(ap512):
            return ap512.rearrange("p (q r) -> p r q", r=4)

        kvst_keys = [self.pg(KVST + 2048 * i, 2048) for i in range(4)]
        p_local = t * TT

        def out_kv(need_out, g, blkidx):
            if g == 0:
                return self.kv128_p[l, 0:128, :]
            if g == 1:
                return self.kv512_p[l, blkidx * 128:(blkidx + 1) * 128, :]
            return self.kv2048_p[l, t * TT + blkidx * 128:t * TT + (blkidx + 1) * 128, :]

        for g in range(3):
            wb, wk, _ = self.ring_get()
            wv = wb[:, :].rearrange("p (k n) -> p k n", n=512)
            for c in range(4):
                b = self.next_bank()
                self.mm_group(b, [(self.psb[b][:, :], wv[:, k, c * 128:(c + 1) * 128], hT[:, k, :], wk + hTk(k)) for k in range(KC)])
                src = self.psb[b][:, :] if g == 0 else m16_view(self.psb[b][:, :])
                dst = qT[:, g * 4 + c, :] if g == 0 else qT[:, g * 4 + c, :].rearrange("p (r q) -> p r q", r=4)
                copy_op(ev_eng(), dst, src, [("ps", b)], self.pg(QT + 1024 * (g * 4 + c), 1024))
                if samp:
                    b2 = self.next_bank()
                    self.mm_group(b2, [(self.psb[b2][:, 0:NS], wv[:, k, c * 128:(c + 1) * 128], self.hT_s[:, k, :], wk + ["hTs"]) for k in range(KC)])
                    copy_op("act", self.qT_s[:, g * 4 + c, :], self.psb[b2][:, 0:NS], [("ps", b2)], ["qTs"])
            wb, wk, _ = self.ring_get()
            wv = wb[:, :].rearrange("p (k n) -> p k n", n=512)
            for c in range(4):
                b = self.next_bank()
                self.mm_group(b, [(self.psb[b][:, :], wv[:, k, c * 128:(c + 1) * 128], hT[:, k, :], wk + hTk(k)) for k in range(KC)])
                if g == 0:
                    copy_op(ev_eng(), self.kT1[:, c, 128:640], self.psb[b][:, :], [("ps", b)], [("kT1", c)])
                elif g == 1:
                    copy_op(ev_eng(), self.kT2[:, c, par, :].rearrange("p (r q) -> p r q", r=4), m16_view(self.psb[b][:, :]),
                            [("ps", b)], [("kT2", c, par)])
                else:
                    copy_op(ev_eng(), self.kT3[:, c, :].rearrange("p (r q) -> p r q", r=4), m16_view(self.psb[b][:, :]),
                            [("ps", b)], [("kT3", c)])
                if samp:
                    b2 = self.next_bank()
                    self.mm_group(b2, [(self.psb[b2][:, 0:NS], wv[:, k, c * 128:(c + 1) * 128], self.hT_s[:, k, :], wk + ["hTs"]) for k in range(KC)])
                    copy_op("act", self.kT_s[:, g * 4 + c, :], self.psb[b2][:, 0:NS], [("ps", b2)], ["kTs"])
            need_out = [j for j in range(4) if (g == 2) or (t == 3 and (g == 1 or j == 3))]
            lhs_of = lambda j: ((lambda k: hT[:, k, j * 128:(j + 1) * 128]) if g == 0 else
                                (lambda k: hT[:, k, :].rearrange("p (q r) -> p r q", r=4)[:, j]))
            for j in need_out:
                si = self.kvst_i % 4
                self.kvst_i += 1
                b = self.next_bank()
                lhs = lhs_of(j)
                self.mm_group(b, [(self.psb[b][:, :], lhs(k), wv[:, k, :], wk + hTk(k)) for k in range(KC)])
                copy_op(ev_eng(), kvst[si][:, :], self.psb[b][:, :], [("ps", b)], kvst_keys[si])
                sy.dma("sp", out_kv(True, g, j)[:, 0:512], kvst[si][:, :], self.ds(f"okv{si}", True), reads=kvst_keys[si])
            if samp:
                si = self.kvst_i % 4
                self.kvst_i += 1
                b2 = self.next_bank()
                self.mm_group(b2, [(self.psb[b2][0:NS, :], self.hT_s[:, k, :], wv[:, k, :], wk + ["hTs"]) for k in range(KC)])
                copy_op("act", kvst[si][0:NS, :], self.psb[b2][0:NS, :], [("ps", b2)], kvst_keys[si])
                dst = (self.kv128_s[l, 120:128, 0:512], self.kv512_s[l, 504:512, 0:512], self.kv2048_s[l, 2040:2048, 0:512])[g]
                sy.dma("sp", dst, kvst[si][0:NS, :], self.ds(f"okv{si}", True), reads=kvst_keys[si])
            wb, wk, _ = self.ring_get()
            wv = wb[:, :].rearrange("p (k n) -> p k n", n=512)
            for j in range(4):
                lhs = lhs_of(j)
                b = self.next_bank()
                self.mm_group(b, [(self.psb[b][:, :], lhs(k), wv[:, k, :], wk + hTk(k)) for k in range(KC)])
                if g == 0:
                    copy_op("act", self.V1[:, 1 + j, :], self.psb[b][:, :], [("ps", b)], [("V1", 1 + j)])
                elif g == 1:
                    copy_op("act", self.V2[:, par, j, :], self.psb[b][:, :], [("ps", b)], [("V2", par, j)])
                else:
                    copy_op("act", self.V3[:, j, :], self.psb[b][:, :], [("ps", b)], [("V3", j)])
                if j in need_out:
                    si = self.kvst_i % 4
                    self.kvst_i += 1
                    copy_op("dve", kvst[si][:, :], self.psb[b][:, :], [("ps", b)], kvst_keys[si])
                    sy.dma("sp", out_kv(True, g, j)[:, 512:1024], kvst[si][:, :], self.ds(f"okv{si}", True), reads=kvst_keys[si])
            if samp:
                si = self.kvst_i % 4
                self.kvst_i += 1
                b2 = self.next_bank()
                self.mm_group(b2, [(self.psb[b2][0:NS, :], self.hT_s[:, k, :], wv[:, k, :], wk + ["hTs"]) for k in range(KC)])
                copy_op("act", kvst[si][0:NS, :], self.psb[b2][0:NS, :], [("ps", b2)], kvst_keys[si])
                copy_op("dve", self.V_s[0:NS, g, :], self.psb[b2][0:NS, :], [("ps", b2)], [("Vs", g)])
                dst = (self.kv128_s[l, 120:128, 512:1024], self.kv512_s[l, 504:512, 512:1024], self.kv2048_s[l, 2040:2048, 512:1024])[g]
                sy.dma("sp", dst, kvst[si][0:NS, :], self.ds(f"okv{si}", True), reads=kvst_keys[si])
        if t < 3:
            sy.dma("sp", self.hsc[l, t, :, 0:2048], self.kT3[:].rearrange("p h t -> p (h t)"), self.ds("hsc0", True),
                   reads=[("kT3", c) for c in range(4)], writes=[("hsc", l, t, 0)])
            sy.dma("sp", self.hsc[l, t, :, 2048:4096], self.V3[:].rearrange("p j t -> p (j t)"), self.ds("hsc1", True),
                   reads=[("V3", j) for j in range(4)], writes=[("hsc", l, t, 1)])

        if l == 0 and t == 0:
            self.dbg("qT", qT, [P, 12, TT], BF16, self.pg(QT, 12288))
            self.dbg("kT1", self.kT1[:], [P, 4, 640], BF16, [("kT1", c) for c in range(4)])
            self.dbg("kT3", self.kT3[:], [P, 4, 512], BF16, [("kT3", c) for c in range(4)])
            self.dbg("V1", self.V1[:], [P, 5, 512], BF16, [("V1", c) for c in range(5)])
            self.dbg("V3", self.V3[:], [P, 4, 512], BF16, [("V3", c) for c in range(4)])
            self.dbg("ycat", ycat, [P, KC, TT], BF16, self.pg(YC, 16384))

        self.stop("proj")
        hist = {}
        if t >= 1:
            grps = [(0, 1), (2,)] if t == 3 else [tuple(range(t))]
            got = self.ring_get(len(grps))
            if len(grps) == 1:
                got = [got]
            for grp, (hb, hk, _) in zip(grps, got):
                for j, tp in enumerate(grp):
                    hist[tp] = (hb[:, j * 4096:(j + 1) * 4096], hk)
        if samp:
            (cKb, cKk, _), (cVb, cVk, _) = self.ring_get(2)
        LT, OT, RL = KVST, KVST + 2048, KVST + 4096
        Lt = self.av(LT, 2048, F32)
        Ot = self.av(OT, 2048, F32)
        rL = self.av(RL, 2048, F32)
        SB = [0, 1, 2, 3]
        O_NAT, O_M16, L_NAT, L_M16 = 4, 5, 6, 7
        sbank = [0]
        scale = float(128 ** -0.5)

        head_tiles = {}

        def score_phase(h):
            ptb = ptbs[h % 2]
            PTh = PT + 8192 * (h % 2)
            tiles = []
            for half in range(2):
                subs = []
                for qi in range(2):
                    qb = half * 2 + qi
                    qap = qT[:, 0 * 4 + h, qb * 128:(qb + 1) * 128]
                    subs.append((self.kT1[:, h, qb * 128:(qb + 1) * 128], [("kT1", h)], qap, self.V1[:, qb, h * 128:(h + 1) * 128], [("V1", qb)],
                                 h * 2 + 1, "nat", qb * 128, not (t == 0 and qb == 0)))
                    subs.append((self.kT1[:, h, 128 + qb * 128:128 + (qb + 1) * 128], [("kT1", h)], qap, self.V1[:, 1 + qb, h * 128:(h + 1) * 128], [("V1", 1 + qb)],
                                 h * 2 + 0, "nat", qb * 128, True))
                tiles.append(subs)
            for kind in (0, 1):
                pp = par if kind == 0 else 1 - par
                subs = []
                for r in range(4):
                    qap = qT[:, 4 + h, r * 128:(r + 1) * 128]
                    subs.append((self.kT2[:, h, pp, r * 128:(r + 1) * 128], [("kT2", h, pp)], qap, self.V2[:, pp, r, h * 128:(h + 1) * 128], [("V2", pp, r)],
                                 8 + h * 2 + kind, "m16", r * 128, not (kind == 1 and t == 0)))
                tiles.append(subs)
            for tp in range(t, -1, -1):
                subs = []
                for r in range(4):
                    qap = qT[:, 8 + h, r * 128:(r + 1) * 128]
                    if tp == t:
                        kap, kk = self.kT3[:, h, r * 128:(r + 1) * 128], [("kT3", h)]
                        vap, vk = self.V3[:, r, h * 128:(h + 1) * 128], [("V3", r)]
                    else:
                        hb, hk = hist[tp]
                        kap, kk = hb[:, h * 512 + r * 128:h * 512 + (r + 1) * 128], hk
                        vap, vk = hb[:, 2048 + r * 512 + h * 128:2048 + r * 512 + (h + 1) * 128], hk
                    subs.append((kap, kk, qap, vap, vk, 16 + h * 4 + (t - tp), "m16", r * 128, True))
                tiles.append(subs)
            for ti, subs in enumerate(tiles):
                b = SB[sbank[0] % 4]
                sbank[0] += 1
                key = ("ps", b)
                sy.deps("pe", (), [key])
                for si, (kap, kk, qap, vap, vk, mi, tgt, col, valid) in enumerate(subs):
                    last = si == 3
                    qkey = self.pg(QT, 12288)
                    sy.op("pe", lambda e, b=b, si=si, kap=kap, qap=qap: e.matmul(self.psb[b][:, si * 128:(si + 1) * 128], kap, qap, start=True, stop=True),
                          reads=kk + qkey, writes=[key] if last else (), inc=last)
                pkeys = self.pg(PTh + 1024 * ti, 1024)
                sy.op("act", lambda e, b=b, ti=ti: e.activation(out=ptb[:, ti, :], in_=self.psb[b][:, :], func=AF.Exp, scale=scale),
                      reads=[key], writes=pkeys)
                for si, (kap, kk, qap, vap, vk, mi, tgt, col, valid) in enumerate(subs):
                    sy.op("dve", lambda e, ti=ti, si=si, mi=mi: e.tensor_tensor(out=ptb[:, ti, si * 128:(si + 1) * 128], in0=ptb[:, ti, si * 128:(si + 1) * 128],
                                                                                in1=self.masks[:, mi, :], op=ALU.mult),
                          reads=pkeys + ["masks"], writes=pkeys)
            head_tiles[h] = tiles

        for h in range(4):
            score_phase(h)
            tiles = head_tiles[h]
            ptb = ptbs[h % 2]
            PTh = PT + 8192 * (h % 2)
            started = {"nat": False, "m16": False}
            all_subs = [(ti, si, s) for ti, subs in enumerate(tiles) for si, s in enumerate(subs) if s[8]]
            cnt = {"nat": sum(1 for x in all_subs if x[2][6] == "nat"), "m16": sum(1 for x in all_subs if x[2][6] == "m16")}
            seen = {"nat": 0, "m16": 0}
            for ti, si, (kap, kk, qap, vap, vk, mi, tgt, col, valid) in all_subs:
                ob, lb = (O_NAT, L_NAT) if tgt == "nat" else (O_M16, L_M16)
                seen[tgt] += 1
                last = seen[tgt] == cnt[tgt]
                st = not started[tgt]
                started[tgt] = True
                pkeys = self.pg(PTh + 1024 * ti, 1024)
                pap = ptb[:, ti, si * 128:(si + 1) * 128]
                self.mm_raw(ob, self.psb[ob][:, col:col + 128], vap, pap, vk + pkeys, st, last)
                self.mm_raw(lb, self.psb[lb][:, col:col + 128], self.ones[:], pap, ["ones"] + pkeys, st, last)
            kl, ko, kr = self.pg(LT, 2048), self.pg(OT, 2048), self.pg(RL, 2048)
            sy.op("dve", lambda e: e.tensor_copy(out=Lt[:, :].rearrange("p (q r) -> p q r", r=4),
                                                 in_=self.psb[L_M16][:, :].rearrange("p (r q) -> p q r", r=4)),
                  reads=[("ps", L_M16)], writes=kl)
            sy.op("dve", lambda e: e.tensor_tensor(out=Lt[:, :], in0=self.psb[L_NAT][:, :], in1=Lt[:, :], op=ALU.add),
                  reads=[("ps", L_NAT)] + kl, writes=kl)
            sy.op("dve", lambda e: e.reciprocal(out=rL[:, :], in_=Lt[:, :]), reads=kl, writes=kr)
            sy.op("act", lambda e: e.activation(out=Ot[:, :].rearrange("p (q r) -> p q r", r=4),
                                                in_=self.psb[O_M16][:, :].rearrange("p (r q) -> p q r", r=4), func=AF.Copy),
                  reads=[("ps", O_M16)], writes=ko)
            sy.op("dve", lambda e: e.tensor_tensor(out=Ot[:, :], in0=self.psb[O_NAT][:, :], in1=Ot[:, :], op=ALU.add),
                  reads=[("ps", O_NAT)] + ko, writes=ko)
            sy.op("dve", lambda e, h=h: e.tensor_tensor(out=ycat[:, 4 + h, :], in0=Ot[:, :], in1=rL[:, :], op=ALU.mult),
                  reads=ko + kr, writes=self.pg(YC + 1024 * (4 + h), 1024))

            if samp:
                cK = cKb[:, 0:6656].rearrange("p (h r) -> p h r", h=4)
                cV = cVb[:, 0:6656].rearrange("p (b c) -> p b c", c=512)
                b = SB[sbank[0] % 4]
                sbank[0] += 1
                key = ("ps", b)
                sy.deps("pe", (), [key])
                jobs = [(cK[:, h, 0:128], self.qT_s[:, h, 0:8], 0, 8)]
                for r in range(4):
                    jobs.append((cK[:, h, 128 + r * 128:128 + (r + 1) * 128], self.qT_s[:, 4 + h, :].rearrange("p (j r) -> p r j", r=4)[:, r, :], 8 + 2 * r, 2))
                for n in range(8):
                    jobs.append((cK[:, h, 640 + n * 128:640 + (n + 1) * 128], self.qT_s[:, 8 + h, n:n + 1], 16 + n, 1))
                for ji, (kap, qap, c0, w) in enumerate(jobs):
                    sy.op("pe", lambda e, b=b, kap=kap, qap=qap, c0=c0, w=w: e.matmul(self.psb[b][:, c0:c0 + w], kap, qap, start=True, stop=True),
                          reads=cKk + ["qTs"], writes=(), inc=False)
                for g in range(3):
                    lastj = g == 2
                    sy.op("pe", lambda e, b=b, g=g: e.matmul(self.psb[b][0:NS, 24 + 8 * g:32 + 8 * g], self.kT_s[:, g * 4 + h, :], self.qT_s[:, g * 4 + h, :],
                                                               start=True, stop=True),
                          reads=["kTs", "qTs"], writes=[key] if lastj else (), inc=lastj)
                sy.op("act", lambda e, b=b, h=h: e.activation(out=self.pts[:, h, 0:24], in_=self.psb[b][:, 0:24], func=AF.Exp, scale=scale),
                      reads=[key], writes=[("pts", h)])
                sy.op("act", lambda e, b=b, h=h: e.activation(out=self.pts[0:NS, h, 24:48], in_=self.psb[b][0:NS, 24:48], func=AF.Exp, scale=scale),
                      reads=[key], writes=[("pts", h)])
                sy.op("dve", lambda e, h=h: e.tensor_tensor(out=self.pts[:, h, 0:24], in0=self.pts[:, h, 0:24], in1=self.smc[:, h, :], op=ALU.mult),
                      reads=[("pts", h), "smc"], writes=[("pts", h)])
                sy.op("dve", lambda e, h=h: e.tensor_tensor(out=self.pts[0:NS, h, 24:48], in0=self.pts[0:NS, h, 24:48], in1=self.smn[0:NS, h, :], op=ALU.mult),
                      reads=[("pts", h), "smn"], writes=[("pts", h)])
                ob, lb = O_NAT, L_NAT
                pv = [(cV[:, 0, h * 128:(h + 1) * 128], 128, self.pts[:, h, 0:8], slice(0, 8), cVk)]
                for r in range(4):
                    pv.append((cV[:, 1 + r, h * 128:(h + 1) * 128], 128, self.pts[:, h, 8 + 2 * r:10 + 2 * r], ("r", r), cVk))
                for n in range(8):
                    pv.append((cV[:, 5 + n, h * 128:(h + 1) * 128], 128, self.pts[:, h, 16 + n:17 + n], slice(n, n + 1), cVk))
                for g in range(3):
                    pv.append((self.V_s[0:NS, g, h * 128:(h + 1) * 128], NS, self.pts[0:NS, h, 24 + 8 * g:32 + 8 * g], slice(0, 8), [("Vs", g)]))
                for pi, (vap, kp_, pap, osel, vk) in enumerate(pv):
                    last = pi == len(pv) - 1
                    for bank, lhs, lk in ((ob, vap, vk), (lb, self.ones[0:kp_, :], ["ones"])):
                        if isinstance(osel, tuple):
                            oap = self.psb[bank][:, 0:8].rearrange("p (j r) -> p r j", r=4)[:, osel[1], :]
                        else:
                            oap = self.psb[bank][:, osel]
                        self.mm_raw(bank, oap, lhs, pap, lk + [("pts", h)], pi == 0, last)
                sy.op("dve", lambda e: e.reciprocal(out=self.t_rl[:], in_=self.psb[lb][:, 0:8]), reads=[("ps", lb)], writes=["tmps_rl"])
                sy.op("dve", lambda e, h=h: e.tensor_tensor(out=self.ycat_s[:, 4 + h, :], in0=self.psb[ob][:, 0:8], in1=self.t_rl[:], op=ALU.mult),
                      reads=[("ps", ob), "tmps_rl"], writes=["ycats"])

        for h in range(4):
            sy.op("dve", lambda e, h=h: e.tensor_copy(out=self.kT1[:, h, 0:128], in_=self.kT1[:, h, 512:640]), reads=[("kT1", h)], writes=[("kT1", h)])
        sy.op("dve", lambda e: e.tensor_copy(out=self.V1[:, 0, :], in_=self.V1[:, 4, :]), reads=[("V1", 4)], writes=[("V1", 0)])

        self.stop("attn")
        self.norm_gbc_load(l, 1)
        for n in range(4):
            wb, wk, _ = self.ring_get()
            wv = wb[:, :].rearrange("p (k n) -> p k n", n=512)
            for j in range(4):
                b = self.next_bank()
                self.mm_group(b, [(self.psb[b][:, :], ycat[:, k, j * 128:(j + 1) * 128], wv[:, k, :], wk + self.pg(YC + 1024 * k, 1024)) for k in range(KC)])
                sy.op("dve", lambda e, b=b, j=j, n=n: e.tensor_tensor(out=self.x[:, j, n * 512:(n + 1) * 512], in0=self.psb[b][:, :],
                                                                      in1=self.x[:, j, n * 512:(n + 1) * 512], op=ALU.add),
                      reads=[("ps", b), ("x", j)], writes=[("x", j)])
            if samp:
                for c in range(4):
                    b = self.next_bank()
                    self.mm_group(b, [(self.psb[b][:, 0:NS], wv[:, k, c * 128:(c + 1) * 128], self.ycat_s[:, k, :], wk + ["ycats"]) for k in range(KC)])
                    sy.op("dve", lambda e, b=b, c=c, n=n: e.tensor_tensor(out=self.xs[:, 4 * n + c, :], in0=self.psb[b][:, 0:NS],
                                                                          in1=self.xs[:, 4 * n + c, :], op=ALU.add),
                          reads=[("ps", b), "xs"], writes=["xs"])

        self.stop("out")
        hT = self.norm_tm(l, 1)
        if samp:
            rs = self.norm_sample(None)
            for k in range(KC):
                sy.op("dve", lambda e, k=k: e.tensor_tensor(out=self.t_t[:], in0=self.xs[:, k, :], in1=rs, op=ALU.mult),
                      reads=["xs", "tmps_rs"], writes=["tmps_t"])
                sy.op("dve", lambda e, k=k: e.tensor_scalar(out=self.hT_s[:, k, :], in0=self.t_t[:],
                                                            scalar1=self.g12_sb[:, l, 1, k:k + 1], scalar2=None, op0=ALU.mult),
                      reads=["tmps_t", "g12"], writes=["hTs"])
        if l == 1:
            sy.dma("sp", self.av(53248, 8192, F32)[:, :], self.gfin_bc, self.ds("ldf"), writes=self.pg(53248, 8192))
        ACT0, GT, CV = 16384, 16384 + 22528, 16384 + 22528 + 4128
        actb = [self.av(ACT0 + 11264 * i, 11264, BF16, "p (c t) -> p c t", t=TT) for i in range(2)]
        gt = [self.av(GT + 2064 * i, 2056, F32) for i in range(2)]
        gts = self.t_gt[:]
        cvs = self.t_cv[:]
        cnt_ch = [0]

        cvb = [self.av(CV + 2048 * i, 2048, F32) for i in range(4)]
        cvs4 = [self.t_cv4[i][:] for i in range(4)]

        def gate_up(gi):
            c0, c1 = FGROUPS[gi]
            c = c0
            while c < c1:
                ce = min(c + 4, c1)
                nc_ = ce - c
                variants = [(TT, lambda k: hT[:, k, :], hTk, self.gh[0], "gh0")]
                if samp:
                    variants.append((NS, lambda k: self.hT_s[:, k, :], lambda k: ["hTs"], self.gh[1], "gh1"))
                wgb, wgk, _ = self.ring_get()
                wg = wgb[:, 0:KC * nc_ * 128].rearrange("p (k n) -> p k n", n=nc_ * 128)
                for cc in range(nc_):
                    cg = c + cc
                    for vi, (N, rhs, rk, hist_, hk_) in enumerate(variants):
                        if vi == 0:
                            gtb, kgt = gt[cnt_ch[0] % 2], self.pg(GT + 2064 * (cnt_ch[0] % 2), 2056)
                            cb, kcv = cvb[cc], self.pg(CV + 2048 * cc, 2048)
                            cnt_ch[0] += 1
                        else:
                            gtb, kgt = gts, ["tmps_gt"]
                            cb, kcv = cvs4[cc], [("tmps_cv", cc)]
                        bg = self.next_bank()
                        self.mm_group(bg, [(self.psb[bg][:, 0:N], wg[:, k, cc * 128:(cc + 1) * 128], rhs(k), wgk + rk(k)) for k in range(KC)])
                        copy_op("act", gtb[:, 2:N + 2], self.psb[bg][:, 0:N], [("ps", bg)], kgt)
                        sy.op("dve", lambda e: e.tensor_copy(out=gtb[:, 0:2], in_=hist_[:, cg, :]), reads=[hk_], writes=kgt)
                        sy.op("dve", lambda e: e.tensor_copy(out=hist_[:, cg, :], in_=gtb[:, N:N + 2]), reads=kgt, writes=[hk_])
                        sy.op("act", lambda e: e.activation(out=cb[:, 0:N], in_=gtb[:, 0:N], func=AF.Copy, scale=self.fcw_sb[:, l, 0, cg:cg + 1]),
                              reads=kgt + ["fcw"], writes=kcv)
                        sy.op("dve", lambda e: e.scalar_tensor_tensor(out=cb[:, 0:N], in0=gtb[:, 1:N + 1], scalar=self.fcw_sb[:, l, 1, cg:cg + 1],
                                                                      in1=cb[:, 0:N], op0=ALU.mult, op1=ALU.add),
                              reads=kgt + kcv + ["fcw"], writes=kcv)
                        sy.op("dve", lambda e: e.scalar_tensor_tensor(out=cb[:, 0:N], in0=gtb[:, 2:N + 2], scalar=self.fcw_sb[:, l, 2, cg:cg + 1],
                                                                      in1=cb[:, 0:N], op0=ALU.mult, op1=ALU.add),
                              reads=kgt + kcv + ["fcw"], writes=kcv)
                        sy.op("act", lambda e: e.activation(out=cb[:, 0:N], in_=cb[:, 0:N], func=AF.Silu), reads=kcv, writes=kcv)
                wub, wuk, _ = self.ring_get()
                wu = wub[:, 0:KC * nc_ * 128].rearrange("p (k n) -> p k n", n=nc_ * 128)
                for cc in range(nc_):
                    cg = c + cc
                    ci = cg - c0
                    for vi, (N, rhs, rk, hist_, hk_) in enumerate(variants):
                        if vi == 0:
                            cb, kcv = cvb[cc], self.pg(CV + 2048 * cc, 2048)
                            aout, kao = actb[gi % 2][:, ci, :], self.pg(ACT0 + 11264 * (gi % 2) + 1024 * ci, 1024)
                        else:
                            cb, kcv = cvs4[cc], [("tmps_cv", cc)]
                            aout, kao = self.act_s[:, cg, :], ["acts"]
                        bu = self.next_bank()
                        self.mm_group(bu, [(self.psb[bu][:, 0:N], wu[:, k, cc * 128:(cc + 1) * 128], rhs(k), wuk + rk(k)) for k in range(KC)])
                        sy.op("dve", lambda e: e.tensor_tensor(out=aout, in0=self.psb[bu][:, 0:N], in1=cb[:, 0:N], op=ALU.mult),
                              reads=[("ps", bu)] + kcv, writes=kao)
                c = ce

        def down(gi):
            c0, c1 = FGROUPS[gi]
            nch = c1 - c0
            for n in range(4):
                wb, wk, _ = self.ring_get()
                wv = wb[:, 0:nch * 512].rearrange("p (k n) -> p k n", n=512)
                for j in range(4):
                    b = self.next_bank()
                    self.mm_group(b, [(self.psb[b][:, :], actb[gi % 2][:, ci, j * 128:(j + 1) * 128], wv[:, ci, :],
                                       wk + self.pg(ACT0 + 11264 * (gi % 2) + 1024 * ci, 1024)) for ci in range(nch)])
                    sy.op("dve", lambda e, b=b, j=j, n=n: e.tensor_tensor(out=self.x[:, j, n * 512:(n + 1) * 512], in0=self.psb[b][:, :],
                                                                          in1=self.x[:, j, n * 512:(n + 1) * 512], op=ALU.add),
                          reads=[("ps", b), ("x", j)], writes=[("x", j)])
                if samp:
                    for m in range(4):
                        b = self.next_bank()
                        self.mm_group(b, [(self.psb[b][:, 0:NS], wv[:, ci, m * 128:(m + 1) * 128], self.act_s[:, c0 + ci, :], wk + ["acts"]) for ci in range(nch)])
                        sy.op("dve", lambda e, b=b, m=m, n=n: e.tensor_tensor(out=self.xs[:, 4 * n + m, :], in0=self.psb[b][:, 0:NS],
                                                                              in1=self.xs[:, 4 * n + m, :], op=ALU.add),
                              reads=[("ps", b), "xs"], writes=["xs"])

        gate_up(0); gate_up(1); down(0); gate_up(2); down(1); gate_up(3); down(2); down(3)

        self.stop("ffn")
        if l == 0:
            for j in range(4):
                sy.dma("sp", self.xsc[t, :, j * D:(j + 1) * D], self.x[:, j, :], self.ds(f"osc{j}", True), reads=[("x", j)], writes=[("xsc", t, j)])
        else:
            GF, JK, YS = 53248, 61440, 16384
            gf = self.av(GF, 8192, F32)
            junk = self.av(JK, 4096, BF16)
            yst = [self.av(YS + 8192 * i, 8192, F32) for i in range(3)]
            for j in range(4):
                ysk = self.pg(YS + 8192 * (j % 3), 8192)
                sy.op("act", lambda e, j=j: e.activation(out=junk[:, :], in_=self.x[:, j, :], func=AF.Square,
                                                         accum_out=self.stat[:, j:j + 1]),
                      reads=[("x", j)], writes=self.pg(JK, 4096) + [("stat", j)])
                sy.op("act", lambda e, j=j: e.activation(out=self.stat[:, 4 + j:5 + j], in_=self.stat[:, j:j + 1], func=AF.Sqrt,
                                                         bias=self.epsb[:, 0:1], scale=1.0 / D),
                      reads=[("stat", j), "epsb"], writes=[("stat", 4 + j)])
                sy.op("dve", lambda e, j=j: e.reciprocal(out=self.stat[:, 8 + j:9 + j], in_=self.stat[:, 4 + j:5 + j]),
                      reads=[("stat", 4 + j)], writes=[("stat", 8 + j)])
                sy.op("dve", lambda e, j=j: e.scalar_tensor_tensor(out=yst[j % 3][:, :], in0=self.x[:, j, :], scalar=self.stat[:, 8 + j:9 + j],
                                                                   in1=gf[:, :], op0=ALU.mult, op1=ALU.mult),
                      reads=[("x", j), ("stat", 8 + j)] + self.pg(GF, 8192), writes=ysk)
                sy.dma("sp", self.y_p[t * TT + j * P:t * TT + (j + 1) * P, :], yst[j % 3][:, :], self.ds(f"oy{j % 3}", True), reads=ysk)
            if samp:
                rs = self.norm_sample(None)
                yo = self.t_yo[:]
                for k in range(KC):
                    sy.op("dve", lambda e, k=k: e.tensor_tensor(out=self.t_t[:], in0=self.xs[:, k, :], in1=rs, op=ALU.mult),
                          reads=["xs", "tmps_rs"], writes=["tmps_t"])
                    sy.op("dve", lambda e, k=k: e.tensor_scalar(out=yo[:, k, :], in0=self.t_t[:],
                                                                scalar1=self.gfin_sb[:, k:k + 1], scalar2=None, op0=ALU.mult),
                          reads=["tmps_t", "gfin"], writes=["tmps_yo"])
                sy.dma("sp", self.y_s, yo, self.ds("oys", True), reads=["tmps_yo"])


_CACHE = {}


def _get_nc():
    if "nc" not in _CACHE:
        b = Builder()
        _CACHE["nc"] = b.build()
    return _CACHE["nc"]


def _fm(v, nchunk):
    return v


def prepare_inputs(inputs, core):
    f = lambda a: np.ascontiguousarray(np.asarray(a, dtype=np.float32))
    b, s = core // 2, core
    m = {}
    m["xp"] = f(inputs["x_prompt"][b])
    m["xs_fm"] = f(np.asarray(inputs["x_sample"][s]).reshape(NS, KC, P).transpose(2, 1, 0))
    for k in ("w_in", "w_out", "w_gate", "w_up", "w_down", "pool_w"):
        m[k] = f(inputs[k])
    n1 = np.asarray(inputs["norm1"]).reshape(DEPTH, KC, P)
    n2 = np.asarray(inputs["norm2"]).reshape(DEPTH, KC, P)
    m["g12"] = f(np.stack([n1, n2], axis=1).transpose(3, 0, 1, 2))
    fn = np.asarray(inputs["final_norm"])
    m["gfin_bc"] = f(np.broadcast_to(fn[None, :], (P, D)))
    g2 = np.stack([np.asarray(inputs["norm1"]), np.asarray(inputs["norm2"])], axis=1)
    m["gbc12"] = f(np.broadcast_to(g2[:, :, None, :], (DEPTH, 2, P, D)))
    m["gfin_fm"] = f(fn.reshape(KC, P).T)
    m["caw"] = f(np.asarray(inputs["conv_a_w"]).reshape(DEPTH, 3, 4, P).transpose(3, 0, 1, 2))
    m["psc"] = f(np.asarray(inputs["pool_scale"]).reshape(DEPTH, 8, P).transpose(2, 0, 1))
    m["fcw"] = f(np.asarray(inputs["ffn_conv_w"]).reshape(DEPTH, 3, FC, P).transpose(3, 0, 1, 2))
    m["s_conv"] = f(np.asarray(inputs["state_conv_a"])[:, s].reshape(DEPTH, 2, 4, P).transpose(0, 3, 2, 1))
    m["s_pool"] = f(np.asarray(inputs["state_pool"])[:, s].reshape(DEPTH, 15, 8, P).transpose(0, 3, 2, 1))
    m["s_ffn"] = f(np.asarray(inputs["state_ffn_conv"])[:, s].reshape(DEPTH, 2, FC, P).transpose(0, 3, 2, 1))
    c128 = np.asarray(inputs["cache_kv_w128"])[:, s]
    c512 = np.asarray(inputs["cache_kv_w512"])[:, s]
    c2048 = np.asarray(inputs["cache_kv_w2048"])[:, s]
    m["c128"] = f(c128.reshape(DEPTH, 128, 1024))
    m["c512"] = f(c512.reshape(DEPTH, 512, 1024))
    m["c2048"] = f(c2048.reshape(DEPTH, 2048, 1024))
    rows = [c128]
    for r in range(4):
        rows.append(c512[:, r::4])
    for n in range(8):
        rows.append(c2048[:, n::16])
    blk = np.stack(rows, axis=1)
    kk = blk[:, :, :, 0]
    m["ckT"] = f(kk.transpose(0, 4, 3, 1, 2).reshape(DEPTH, P, 4 * 1664))
    vv = blk[:, :, :, 1]
    m["cV"] = f(vv.transpose(0, 2, 1, 3, 4).reshape(DEPTH, P, 13 * 512))
    M, SC, SN = build_masks()
    m["masks"] = f(M.reshape(P, 32 * 128))
    m["smask_c"] = f(SC.reshape(P, 96))
    m["smask_n"] = f(SN.reshape(P, 96))
    rc = np.zeros((P, 16), np.float32)
    rc[:, :15] = 1.0 / np.arange(1, 16, dtype=np.float32)
    m["rc_tab"] = rc
    m["ident"] = np.eye(P, dtype=np.float32)
    return m


def assemble(results, ncores=8):
    perm = m16_of_natural()
    y_p = np.zeros((4, SEQ, D), np.float32)
    y_s = np.zeros((8, NS, D), np.float32)
    kv128_p = np.zeros((DEPTH, 4, 128, 2, 4, 128), np.float32)
    kv512_p = np.zeros((DEPTH, 4, 512, 2, 4, 128), np.float32)
    kv2048_p = np.zeros((DEPTH, 4, 2048, 2, 4, 128), np.float32)
    kv128_s = np.zeros((DEPTH, 8, 128, 2, 4, 128), np.float32)
    kv512_s = np.zeros((DEPTH, 8, 512, 2, 4, 128), np.float32)
    kv2048_s = np.zeros((DEPTH, 8, 2048, 2, 4, 128), np.float32)
    conv_p = np.zeros((DEPTH, 4, 2, 512), np.float32)
    conv_s = np.zeros((DEPTH, 8, 2, 512), np.float32)
    pool_p = np.zeros((DEPTH, 4, 15, 1024), np.float32)
    pool_s = np.zeros((DEPTH, 8, 15, 1024), np.float32)
    ffn_p = np.zeros((DEPTH, 4, 2, DFF), np.float32)
    ffn_s = np.zeros((DEPTH, 8, 2, DFF), np.float32)
    for c in range(ncores):
        r = results[c]
        s = c
        y_s[s] = np.asarray(r["y_s"]).transpose(2, 1, 0).reshape(NS, D)
        kv128_s[:, s] = np.asarray(r["kv128_s"]).reshape(DEPTH, 128, 2, 4, 128)
        kv512_s[:, s] = np.asarray(r["kv512_s"]).reshape(DEPTH, 512, 2, 4, 128)
        kv2048_s[:, s] = np.asarray(r["kv2048_s"]).reshape(DEPTH, 2048, 2, 4, 128)
        conv_s[:, s] = np.asarray(r["conv_s"]).transpose(0, 3, 2, 1).reshape(DEPTH, 2, 512)
        pool_s[:, s] = np.asarray(r["pool_s"]).transpose(0, 3, 2, 1).reshape(DEPTH, 15, 1024)
        ffn_s[:, s] = np.asarray(r["ffn_s"]).transpose(0, 3, 2, 1).reshape(DEPTH, 2, DFF)
        if c % 2 == 0:
            b = c // 2
            y_p[b] = np.asarray(r["y_p"])
            kv128_p[:, b] = np.asarray(r["kv128_p"]).reshape(DEPTH, 128, 2, 4, 128)
            k5 = np.asarray(r["kv512_p"]).reshape(DEPTH, 512, 2, 4, 128)
            tmp = np.zeros_like(k5)
            tmp[:, perm] = k5
            kv512_p[:, b] = tmp
            k2 = np.asarray(r["kv2048_p"]).reshape(DEPTH, 4, 512, 2, 4, 128)
            tmp = np.zeros_like(k2)
            tmp[:, :, perm] = k2
            kv2048_p[:, b] = tmp.reshape(DEPTH, 2048, 2, 4, 128)
            conv_p[:, b] = np.asarray(r["conv_p"]).transpose(0, 3, 2, 1).reshape(DEPTH, 2, 512)
            pool_p[:, b] = np.asarray(r["pool_p"]).transpose(0, 3, 2, 1).reshape(DEPTH, 15, 1024)
            ffn_p[:, b] = np.asarray(r["ffn_p"]).transpose(0, 3, 2, 1).reshape(DEPTH, 2, DFF)
    return (y_p, y_s, kv128_p, kv128_s, kv512_p, kv512_s, kv2048_p, kv2048_s,
            conv_p, conv_s, pool_p, pool_s, ffn_p, ffn_s)


def kernel(**inputs):
    nc = _get_nc()
    in_maps = [prepare_inputs(inputs, c) for c in range(8)]
    res = run_bass_kernel_spmd(nc, in_maps, core_ids=list(range(8)))
    return assemble(res.results)
```
